# Optimizing a Trainium2 kernel written in Bass

```python
import jax, jax.numpy as jnp
from jax import lax
import numpy as np

D_MODEL = 2048
BATCH = 4
SEQ = 2048
DEPTH = 4
DEC_BATCH = 128
DEC_SEQ = 4
PAST_LEN = 16384
PAGE_SIZE = 128

S5_WIDTH = D_MODEL // 2
S5_GROUP = 16
S5_GROUPS = S5_WIDTH // S5_GROUP
S5_STATE = 64
RG_WIDTH = D_MODEL
RG_BLOCKS = 16
RG_BLOCK = RG_WIDTH // RG_BLOCKS
RG_C = 8.0
CONV_W = 4
IN_COLS = 2 * S5_WIDTH + 2 * RG_WIDTH + 2 * D_MODEL
EPS = 1e-6
DT_MIN = 1e-3
DT_MAX = 1e-1

kernel_name = "hybrid_s5_rglru_gated_step"

_SPLITS = [S5_WIDTH, 2 * S5_WIDTH, 2 * S5_WIDTH + RG_WIDTH, 2 * S5_WIDTH + 2 * RG_WIDTH,
           2 * S5_WIDTH + 2 * RG_WIDTH + D_MODEL]


def _rmsnorm(x, g):
    xf = x.astype(jnp.float32)
    y = xf * lax.rsqrt(jnp.mean(xf * xf, axis=-1, keepdims=True) + EPS)
    return (y * g.astype(jnp.float32)).astype(x.dtype)


def _complex_linear_scan(a_re, a_im, b_re, b_im):
    def combine(l, r):
        a1r, a1i, b1r, b1i = l
        a2r, a2i, b2r, b2i = r
        return (a1r * a2r - a1i * a2i,
                a1r * a2i + a1i * a2r,
                a2r * b1r - a2i * b1i + b2r,
                a2r * b1i + a2i * b1r + b2i)
    return lax.associative_scan(combine, (a_re, a_im, b_re, b_im), axis=1)


def _real_linear_scan(a, b):
    def combine(l, r):
        a1, b1 = l
        a2, b2 = r
        return a1 * a2, a2 * b1 + b2
    return lax.associative_scan(combine, (a, b), axis=1)


def _s5_branch(u, h0_re, h0_im, p):
    f32 = jnp.float32
    bsz, t = u.shape[0], u.shape[1]
    uf = u.astype(f32).reshape(bsz, t, S5_GROUPS, S5_GROUP)
    lam_re = p["s5_lam_re"].astype(f32)
    lam_im = p["s5_lam_im"].astype(f32)
    dt = jnp.exp(p["s5_log_dt"].astype(f32))[:, None]
    mag = jnp.exp(lam_re * dt)
    abar_re = mag * jnp.cos(lam_im * dt)
    abar_im = mag * jnp.sin(lam_im * dt)
    nr = abar_re - 1.0
    ni = abar_im
    den = lam_re * lam_re + lam_im * lam_im
    coef_re = ((nr * lam_re + ni * lam_im) / den)[..., None]
    coef_im = ((ni * lam_re - nr * lam_im) / den)[..., None]
    b_re = p["s5_b_re"].astype(f32)
    b_im = p["s5_b_im"].astype(f32)
    bb_re = coef_re * b_re - coef_im * b_im
    bb_im = coef_re * b_im + coef_im * b_re
    bu_re = jnp.einsum("gpc,btgc->btgp", bb_re, uf)
    bu_im = jnp.einsum("gpc,btgc->btgp", bb_im, uf)
    h0r = h0_re.astype(f32)
    h0i = h0_im.astype(f32)
    bu_re = bu_re.at[:, 0].add(abar_re * h0r - abar_im * h0i)
    bu_im = bu_im.at[:, 0].add(abar_re * h0i + abar_im * h0r)
    a_re = jnp.broadcast_to(abar_re, bu_re.shape)
    a_im = jnp.broadcast_to(abar_im, bu_im.shape)
    _, _, h_re, h_im = _complex_linear_scan(a_re, a_im, bu_re, bu_im)
    y = (jnp.einsum("gcp,btgp->btgc", p["s5_c_re"].astype(f32), h_re)
         - jnp.einsum("gcp,btgp->btgc", p["s5_c_im"].astype(f32), h_im))
    y = y.reshape(bsz, t, S5_WIDTH) + p["s5_d"].astype(f32) * u.astype(f32)
    y = jax.nn.gelu(y, approximate=False)
    y = y * jax.nn.sigmoid(y @ p["s5_w_glu"].astype(f32) + p["s5_b_glu"].astype(f32))
    return y.astype(u.dtype), h_re[:, -1], h_im[:, -1]


def _rglru_branch(xb, h0, conv_buf, p):
    f32 = jnp.float32
    bsz, t = xb.shape[0], xb.shape[1]
    xcat = jnp.concatenate([conv_buf.astype(xb.dtype), xb], axis=1)
    new_buf = xcat[:, -(CONV_W - 1):]
    w = p["rg_conv_w"]
    conv = p["rg_conv_b"] + sum(w[k] * xcat[:, k:k + t] for k in range(CONV_W))
    xh = conv.reshape(bsz, t, RG_BLOCKS, RG_BLOCK)
    r = jax.nn.sigmoid(jnp.einsum("bthi,hij->bthj", xh, p["rg_w_r"]).reshape(bsz, t, RG_WIDTH)
                       + p["rg_b_r"]).astype(f32)
    gi = jax.nn.sigmoid(jnp.einsum("bthi,hij->bthj", xh, p["rg_w_i"]).reshape(bsz, t, RG_WIDTH)
                        + p["rg_b_i"]).astype(f32)
    log_a = -RG_C * r * jax.nn.softplus(-p["rg_lam"].astype(f32))
    a = jnp.exp(log_a)
    mult = jnp.sqrt(-jnp.expm1(2.0 * log_a))
    b = mult * gi * conv.astype(f32)
    b = b.at[:, 0].add(a[:, 0] * h0.astype(f32))
    _, h = _real_linear_scan(a, b)
    return h.astype(xb.dtype), h[:, -1], new_buf


def _layer(x, c, s5_re0, s5_im0, rg_h0, conv0, p):
    ada = c @ p["w_ada"] + p["b_ada"]
    shift, scale, gate = jnp.split(ada, 3, axis=-1)
    xn = _rmsnorm(x, p["norm_gain"]) * (1.0 + scale[:, None]) + shift[:, None]
    proj = xn @ p["w_in"] + p["b_in"]
    u_a, z_a, x_b, z_b, g_a, g_b = jnp.split(proj, _SPLITS, axis=-1)
    y_a, s5_re, s5_im = _s5_branch(u_a, s5_re0, s5_im0, p)
    y_b, rg_h, conv_buf = _rglru_branch(x_b, rg_h0, conv0, p)
    y_a = y_a * jax.nn.silu(z_a)
    y_b = y_b * jax.nn.silu(z_b)
    merged = (jax.nn.sigmoid(g_a) * (y_a @ p["w_proj_a"])
              + jax.nn.sigmoid(g_b) * (y_b @ p["w_proj_b"]))
    out = merged @ p["w_out"]
    x = x + gate[:, None] * out
    return x, (s5_re, s5_im, rg_h, conv_buf)


def _trunk(x, c, s5_re0, s5_im0, rg_h0, conv0, params, final_gain, state_dtype):
    s5r, s5i, rgh, cnv = [], [], [], []
    for l in range(DEPTH):
        p = {k: v[l] for k, v in params.items()}
        x, (a, b, h, buf) = _layer(x, c, s5_re0[l], s5_im0[l], rg_h0[l], conv0[l], p)
        s5r.append(a.astype(state_dtype))
        s5i.append(b.astype(state_dtype))
        rgh.append(h.astype(state_dtype))
        cnv.append(buf.astype(state_dtype))
    y = _rmsnorm(x, final_gain)
    return y, jnp.stack(s5r), jnp.stack(s5i), jnp.stack(rgh), jnp.stack(cnv)


def setup_inputs(seed: int = 0) -> dict:
    key = jax.random.key(seed)
    ks = iter(jax.random.split(key, 40))
    f32 = jnp.float32

    def nrm(shape, s):
        return jax.random.normal(next(ks), shape, f32) * s

    n_idx = jnp.arange(S5_STATE, dtype=f32)
    u_a = jax.random.uniform(next(ks), (DEPTH, RG_WIDTH), f32, 0.9, 0.999)
    sig = u_a ** (1.0 / RG_C)
    rg_lam = jnp.log(sig) - jnp.log1p(-sig)
    log_dt = jax.random.uniform(next(ks), (DEPTH, S5_GROUPS), f32, np.log(DT_MIN), np.log(DT_MAX))
    return {
        "x_prompt": nrm((BATCH, SEQ, D_MODEL), 1.0),
        "x_sample": nrm((DEC_BATCH, DEC_SEQ, D_MODEL), 1.0),
        "state_s5_re": nrm((DEPTH, DEC_BATCH, S5_GROUPS, S5_STATE), 0.1),
        "state_s5_im": nrm((DEPTH, DEC_BATCH, S5_GROUPS, S5_STATE), 0.1),
        "state_rglru_h": nrm((DEPTH, DEC_BATCH, RG_WIDTH), 0.5),
        "state_conv": nrm((DEPTH, DEC_BATCH, CONV_W - 1, RG_WIDTH), 1.0),
        "c_prompt": nrm((BATCH, D_MODEL), 1.0),
        "c_sample": nrm((DEC_BATCH, D_MODEL), 1.0),
        "w_ada": nrm((DEPTH, D_MODEL, 3 * D_MODEL), 0.5 * D_MODEL ** -0.5),
        "b_ada": nrm((DEPTH, 3 * D_MODEL), 0.01),
        "norm_gain": 1.0 + nrm((DEPTH, D_MODEL), 0.01),
        "w_in": nrm((DEPTH, D_MODEL, IN_COLS), D_MODEL ** -0.5),
        "b_in": nrm((DEPTH, IN_COLS), 0.01),
        "s5_lam_re": -0.5 + nrm((DEPTH, S5_GROUPS, S5_STATE), 0.01),
        "s5_lam_im": jnp.pi * n_idx + nrm((DEPTH, S5_GROUPS, S5_STATE), 0.01),
        "s5_log_dt": log_dt,
        "s5_b_re": nrm((DEPTH, S5_GROUPS, S5_STATE, S5_GROUP), (2 * S5_GROUP) ** -0.5),
        "s5_b_im": nrm((DEPTH, S5_GROUPS, S5_STATE, S5_GROUP), (2 * S5_GROUP) ** -0.5),
        "s5_c_re": nrm((DEPTH, S5_GROUPS, S5_GROUP, S5_STATE), S5_STATE ** -0.5),
        "s5_c_im": nrm((DEPTH, S5_GROUPS, S5_GROUP, S5_STATE), S5_STATE ** -0.5),
        "s5_d": nrm((DEPTH, S5_WIDTH), 1.0),
        "s5_w_glu": nrm((DEPTH, S5_WIDTH, S5_WIDTH), S5_WIDTH ** -0.5),
        "s5_b_glu": nrm((DEPTH, S5_WIDTH), 0.01),
        "rg_conv_w": nrm((DEPTH, CONV_W, RG_WIDTH), CONV_W ** -0.5),
        "rg_conv_b": nrm((DEPTH, RG_WIDTH), 0.01),
        "rg_w_r": nrm((DEPTH, RG_BLOCKS, RG_BLOCK, RG_BLOCK), RG_BLOCK ** -0.5),
        "rg_b_r": nrm((DEPTH, RG_WIDTH), 0.01),
        "rg_w_i": nrm((DEPTH, RG_BLOCKS, RG_BLOCK, RG_BLOCK), RG_BLOCK ** -0.5),
        "rg_b_i": nrm((DEPTH, RG_WIDTH), 0.01),
        "rg_lam": rg_lam,
        "w_proj_a": nrm((DEPTH, S5_WIDTH, D_MODEL), S5_WIDTH ** -0.5),
        "w_proj_b": nrm((DEPTH, RG_WIDTH, D_MODEL), RG_WIDTH ** -0.5),
        "w_out": nrm((DEPTH, D_MODEL, D_MODEL), D_MODEL ** -0.5),
        "final_gain": 1.0 + nrm((D_MODEL,), 0.01),
    }


def reference(x_prompt, x_sample, state_s5_re, state_s5_im, state_rglru_h, state_conv,
              c_prompt, c_sample, w_ada, b_ada, norm_gain, w_in, b_in,
              s5_lam_re, s5_lam_im, s5_log_dt, s5_b_re, s5_b_im, s5_c_re, s5_c_im,
              s5_d, s5_w_glu, s5_b_glu, rg_conv_w, rg_conv_b, rg_w_r, rg_b_r, rg_w_i, rg_b_i,
              rg_lam, w_proj_a, w_proj_b, w_out, final_gain):
    params = {
        "w_ada": w_ada, "b_ada": b_ada, "norm_gain": norm_gain, "w_in": w_in, "b_in": b_in,
        "s5_lam_re": s5_lam_re, "s5_lam_im": s5_lam_im, "s5_log_dt": s5_log_dt,
        "s5_b_re": s5_b_re, "s5_b_im": s5_b_im, "s5_c_re": s5_c_re, "s5_c_im": s5_c_im,
        "s5_d": s5_d, "s5_w_glu": s5_w_glu, "s5_b_glu": s5_b_glu,
        "rg_conv_w": rg_conv_w, "rg_conv_b": rg_conv_b, "rg_w_r": rg_w_r, "rg_b_r": rg_b_r,
        "rg_w_i": rg_w_i, "rg_b_i": rg_b_i, "rg_lam": rg_lam,
        "w_proj_a": w_proj_a, "w_proj_b": w_proj_b, "w_out": w_out,
    }
    sdt = state_s5_re.dtype
    bp = x_prompt.shape[0]
    z_s5 = jnp.zeros((DEPTH, bp, S5_GROUPS, S5_STATE), jnp.float32)
    z_h = jnp.zeros((DEPTH, bp, RG_WIDTH), jnp.float32)
    z_conv = jnp.zeros((DEPTH, bp, CONV_W - 1, RG_WIDTH), x_prompt.dtype)
    y_prompt, s5_re_p, s5_im_p, rg_h_p, conv_p = _trunk(
        x_prompt, c_prompt, z_s5, z_s5, z_h, z_conv, params, final_gain, sdt)
    y_sample, s5_re_s, s5_im_s, rg_h_s, conv_s = _trunk(
        x_sample, c_sample, state_s5_re, state_s5_im, state_rglru_h, state_conv, params, final_gain, sdt)
    return (y_prompt, y_sample, s5_re_p, s5_im_p, rg_h_p, conv_p, s5_re_s, s5_im_s, rg_h_s, conv_s)
```

```python
import numpy as np
import concourse.bass as bass
import concourse.mybir as mybir
from concourse.bass_utils import run_bass_kernel_spmd

F32 = mybir.dt.float32
BF16 = mybir.dt.bfloat16
AF = mybir.ActivationFunctionType
ALU = mybir.AluOpType

D = 2048
KT = 16
DEPTH = 4
S5W = 1024
NG = 64
NP = 64
GC = 16
NQ = 32
RGW = 2048
NH = 16
INCOLS = 10240
LCH = 8
EPS = 1e-6
N_CORES = 8
TWO_PI = float(2 * np.pi)
MAGIC = 12582912.0


def _dsize(dt):
    return mybir.dt.size(dt)


def _box(ap):
    t = ap.tensor
    pat = ap.ap
    off = ap.offset
    esz = _dsize(ap.dtype)
    space = str(getattr(ap, "space", ""))
    name = t.name
    if "DRAM" in space.upper() or "Dram" in type(t).__name__ or "DRam" in type(t).__name__:
        ext = 0
        for st, cnt in pat:
            ext += abs(st) * (cnt - 1)
        return (name, 0, 1, off * esz, (off + ext + 1) * esz)
    row = pat[0][0]
    nparts = pat[0][1]
    if row == 0:
        row = 1 << 40
    p0 = off // row
    f0 = off % row
    ext = 0
    for st, cnt in pat[1:]:
        ext += abs(st) * (cnt - 1)
    f1 = f0 + ext + 1
    b0, b1 = f0 * esz, f1 * esz
    if "PSUM" in space.upper() or "PSum" in type(t).__name__:
        b0 = (b0 // 2048) * 2048
        b1 = ((b1 + 2047) // 2048) * 2048
    return (name, p0, p0 + nparts, b0, b1)


def _ovl(a, b):
    return a[1] < b[2] and b[1] < a[2] and a[3] < b[4] and b[3] < a[4]


def _covers(a, b):
    return a[1] <= b[1] and a[2] >= b[2] and a[3] <= b[3] and a[4] >= b[4]


class Prog:
    ENGS = ("pe", "act", "dve", "pool", "sp")
    KDMA = 8

    def __init__(self, nc):
        self.nc = nc
        self.ops = []
        self.eng_ops = {e: [] for e in self.ENGS}
        self.trk = {}

    def add(self, eng, fn, reads, writes, dma=False):
        oid = len(self.ops)
        deps = {}
        rb = [_box(a) for a in reads]
        wb = [_box(a) for a in writes]
        for b in rb:
            t = self.trk.setdefault(b[0], {"w": [], "r": {}})
            for (wbx, wop) in t["w"]:
                if _ovl(wbx, b):
                    deps[wop] = "RAW"
        for b in wb:
            t = self.trk.setdefault(b[0], {"w": [], "r": {}})
            for (wbx, wop) in t["w"]:
                if _ovl(wbx, b):
                    deps.setdefault(wop, "WAW")
            for (key, rop) in t["r"].items():
                if _ovl(key[1], b):
                    deps.setdefault(rop, "WAR")
        for b in rb:
            t = self.trk[b[0]]
            t["r"][(eng if not dma else ("dma", oid), b)] = oid
        for b in wb:
            t = self.trk[b[0]]
            t["w"] = [(x, o) for (x, o) in t["w"] if not _covers(b, x)]
            t["r"] = {k: o for (k, o) in t["r"].items() if not _covers(b, k[1])}
            t["w"].append((b, oid))
        deps.pop(oid, None)
        op = {"id": oid, "eng": eng, "fn": fn, "dma": dma, "deps": [], "signal": dma}
        for d, kind in deps.items():
            p = self.ops[d]
            if not p["dma"] and p["eng"] == eng:
                if eng == "pe":
                    continue
                if kind != "RAW" and not dma:
                    continue
                if dma and kind != "RAW":
                    pass
            op["deps"].append(d)
            p["signal"] = True
        self.ops.append(op)
        self.eng_ops[eng].append(op)
        return oid

    def emit(self, stack):
        nc = self.nc
        sems = {}
        for e in ("pe", "act", "dve", "pool"):
            sems[e] = stack.enter_context(nc.semaphore("s_" + e))
        dsem = {}
        for q in ("sp", "pool"):
            dsem[q] = [stack.enter_context(nc.semaphore("d_%s%d" % (q, i))) for i in range(self.KDMA)]
        cnt = {e: 0 for e in ("pe", "act", "dve", "pool")}
        dcnt = {"sp": 0, "pool": 0}
        final_dma = {}
        for op in self.ops:
            if op["dma"]:
                q = op["eng"]
                j = dcnt[q]
                dcnt[q] += 1
                s = dsem[q][j % self.KDMA]
                op["sig"] = (s, 16 * (j // self.KDMA + 1))
                op["pre"] = (s, 16 * (j // self.KDMA)) if j >= self.KDMA else None
                final_dma[(q, j % self.KDMA)] = op["sig"]
            elif op["signal"]:
                cnt[op["eng"]] += 1
                op["sig"] = (sems[op["eng"]], cnt[op["eng"]])
        handles = {"pe": "tensor", "act": "scalar", "dve": "vector", "pool": "gpsimd", "sp": "sync"}
        block = stack.enter_context(nc.Block())
        ops = self.ops
        eng_ops = self.eng_ops
        self.nwaits = 0
        prog = self

        def make(engname):
            def body(eng):
                waited = {}
                for op in eng_ops[engname]:
                    need = {}
                    if op["dma"] and op["pre"] is not None:
                        s, v = op["pre"]
                        need[id(s)] = (s, v)
                    for d in op["deps"]:
                        s, v = ops[d]["sig"]
                        if need.get(id(s), (None, 0))[1] < v:
                            need[id(s)] = (s, v)
                    for (s, v) in need.values():
                        if waited.get(id(s), 0) < v:
                            eng.wait_ge(s, v)
                            waited[id(s)] = v
                            prog.nwaits += 1
                    ins = op["fn"](eng)
                    if op["dma"]:
                        ins.then_inc(op["sig"][0], 16)
                    elif op["signal"]:
                        ins.then_inc(op["sig"][0], 1)
                if engname == "sp":
                    for (q, i), (s, v) in final_dma.items():
                        if waited.get(id(s), 0) < v:
                            eng.wait_ge(s, v)
            return body

        for e in self.ENGS:
            getattr(block, handles[e])(make(e))


def unit_list():
    u = []
    for j in range(8):
        u.append(("u", j, 16))
    for j in range(8):
        u.append(("glu", j, 8))
        u.append(("za", j, 16))
    u.append(("xb", 0, 16))
    u.append(("xb", 1, 16))
    for h in range(NH):
        u.append(("zb", h, 16))
        if h + 2 < NH:
            u.append(("xb", h + 2, 16))
    for m in range(KT):
        u.append(("ga", m, 16))
        u.append(("gb", m, 16))
        u.append(("wa", m, 8))
        u.append(("wb", m, 16))
    for m in range(KT):
        u.append(("wo", m, 16))
    return u


UNITS = unit_list()
UNIT_OFF = {}
_o = 0
for (_k, _i, _K) in UNITS:
    UNIT_OFF[(_k, _i)] = (_o, _K)
    _o += _K * 128
WSTREAM_LEN = _o

SM = {}
_o = 0
for _n, _l in [("b_in", 80), ("b_glu", 8), ("gain", 16), ("b_ada", 48), ("dcol", 64), ("conv_w", 64),
               ("conv_b", 16), ("b_r", 16), ("b_i", 16), ("lam", 16)]:
    SM[_n] = (_o, _l)
    _o += _l
SM_LEN = _o
S5R = {}
_o = 0
for _n, _l in [("lam_re", 32), ("lam_im", 32), ("log_dt", 32), ("b_re", 512), ("b_im", 512), ("c_re", 512),
               ("c_im", 512), ("dcol", 64)]:
    S5R[_n] = (_o, _l)
    _o += _l
S5R_LEN = _o


def make_consts(ncmax):
    c = {}
    c["ones"] = np.ones((128, 128), np.float32)
    c["ident"] = np.eye(128, dtype=np.float32)
    r = np.arange(128)
    c["cmask"] = ((r[None, :] // 16) >= (r[:, None] // 16)).astype(np.float32)
    bs = np.zeros((128, 8, 240), np.float32)
    for gl in range(8):
        for cc in range(16):
            bs[gl * 16 + cc, gl, 7 * 16 + cc] = 1.0
    c["selF"] = bs
    bb = np.zeros((128, 8, 240), np.float32)
    for ii in range(8):
        for cc in range(16):
            bb[ii * 16 + cc, ii, 7 * 16 + cc] = 1.0
    c["selB"] = bb
    c["id2"] = np.concatenate([np.eye(64, dtype=np.float32)] * 2, axis=0)
    c["kidx"] = np.tile(np.arange(1, ncmax + 1, dtype=np.float32)[None, :], (128, 1))
    return c


def pack_layer_weights(w_in, w_glu, w_pa, w_pb, w_out):
    out = np.empty((128, WSTREAM_LEN), np.float32)
    win = w_in.reshape(16, 128, INCOLS)
    wglu = w_glu.reshape(8, 128, S5W)
    wpa = w_pa.reshape(8, 128, D)
    wpb = w_pb.reshape(16, 128, D)
    wo = w_out.reshape(16, 128, D)
    colbase = {"u": 0, "za": 1024, "xb": 2048, "zb": 4096, "ga": 6144, "gb": 8192}
    for (kind, idx, K) in UNITS:
        off, _ = UNIT_OFF[(kind, idx)]
        if kind in colbase:
            c0 = colbase[kind] + idx * 128
            blk = win[:, :, c0:c0 + 128]
        elif kind == "glu":
            blk = wglu[:, :, idx * 128:(idx + 1) * 128]
        elif kind == "wa":
            blk = wpa[:, :, idx * 128:(idx + 1) * 128]
        elif kind == "wb":
            blk = wpb[:, :, idx * 128:(idx + 1) * 128]
        else:
            blk = wo[:, :, idx * 128:(idx + 1) * 128]
        out[:, off:off + K * 128] = blk.transpose(1, 0, 2).reshape(128, K * 128)
    return out


def fm16(v):
    return np.ascontiguousarray(v.reshape(-1, 128).T)


def gp_layout(a):
    sh = a.shape
    a = a.reshape(NQ, 2, NP, *sh[2:])
    a = np.moveaxis(a, 0, 2)
    return np.ascontiguousarray(a.reshape(128, NQ, *sh[2:]))


def pack_small(inp, l):
    sm = np.zeros((128, SM_LEN), np.float32)

    def put(name, arr):
        o, n = SM[name]
        sm[:, o:o + n] = arr.reshape(128, n)
    put("b_in", fm16(inp["b_in"][l]))
    put("b_glu", fm16(inp["s5_b_glu"][l]))
    put("gain", fm16(inp["norm_gain"][l]))
    put("b_ada", fm16(inp["b_ada"][l]))
    d = inp["s5_d"][l].reshape(NG, GC)
    dcol = np.broadcast_to(d.T[None, :, :], (LCH, GC, NG)).reshape(128, NG)
    put("dcol", dcol)
    cw = inp["rg_conv_w"][l]
    cwf = np.stack([fm16(cw[k]) for k in range(4)], axis=-1)
    put("conv_w", cwf)
    put("conv_b", fm16(inp["rg_conv_b"][l]))
    put("b_r", fm16(inp["rg_b_r"][l]))
    put("b_i", fm16(inp["rg_b_i"][l]))
    put("lam", fm16(inp["rg_lam"][l]))
    s5 = np.zeros((128, S5R_LEN), np.float32)

    def put5(name, arr):
        o, n = S5R[name]
        s5[:, o:o + n] = arr.reshape(128, n)
    put5("lam_re", gp_layout(inp["s5_lam_re"][l]))
    put5("lam_im", gp_layout(inp["s5_lam_im"][l]))
    put5("log_dt", gp_layout(np.broadcast_to(inp["s5_log_dt"][l][:, None], (NG, NP))))
    put5("b_re", gp_layout(inp["s5_b_re"][l]))
    put5("b_im", gp_layout(inp["s5_b_im"][l]))
    put5("c_re", gp_layout(inp["s5_c_re"][l].transpose(0, 2, 1)))
    put5("c_im", gp_layout(inp["s5_c_im"][l].transpose(0, 2, 1)))
    put5("dcol", dcol)
    return sm, s5


def build(cfg):
    from contextlib import ExitStack
    depth = cfg["depth"]
    PT = cfg["P"]
    NS = cfg["NS"]
    tiles = cfg["tiles"]
    NCOLS_ADA = 1 + NS
    NTMAX = max(np_ + 4 * ns for (_, np_, ns) in tiles)
    NCMAX = max(np_ // LCH for (_, np_, ns) in tiles)
    TOK = PT + 4 * NS
    NSLOT = 4

    nc = bass.Bass("TRN2", target_bir_lowering=False)
    P = Prog(nc)

    def din(name, shape):
        return nc.dram_tensor(name, list(shape), F32, kind="ExternalInput").ap()

    def dout(name, shape):
        return nc.dram_tensor(name, list(shape), F32, kind="ExternalOutput").ap()

    xp = din("xp", [128, KT, PT])
    xs = din("xs", [128, KT, 4 * NS])
    wst = din("wst", [depth, 128, WSTREAM_LEN])
    wada = din("wada", [depth, 128, 48 * 2048])
    cT = din("cT", [128, KT, NCOLS_ADA])
    bada = din("bada", [128, depth, 48])
    smd = din("smd", [depth, 128, SM_LEN])
    s5rd = din("s5rd", [depth, 128, S5R_LEN])
    fgain = din("fgain", [128, KT])
    rgwd = din("rgwd", [depth, 128, 2 * NH * 128])
    st_s5 = din("st_s5", [128, depth, 2, NQ, NS])
    st_h = din("st_h", [128, depth, NH, NS])
    st_cv = din("st_cv", [128, depth, NH, NS, 3])
    c_ones = din("c_ones", [128, 128])
    c_ident = din("c_ident", [128, 128])
    c_cmask = din("c_cmask", [128, 128])
    c_selF = din("c_selF", [128, 8 * 240])
    c_selB = din("c_selB", [128, 8 * 240])
    c_kidx = din("c_kidx", [128, NCMAX])
    c_id2 = din("c_id2", [128, 64])

    yp = dout("yp", [128, KT, PT])
    ys = dout("ys", [128, KT, 4 * NS])
    o_s5p = dout("o_s5p", [128, depth, 2, NQ])
    o_hp = dout("o_hp", [128, depth, NH])
    o_cvp = dout("o_cvp", [128, depth, NH, 3])
    o_s5s = dout("o_s5s", [128, depth, 2, NQ, NS])
    o_hs = dout("o_hs", [128, depth, NH, NS])
    o_cvs = dout("o_cvs", [128, depth, NH, NS, 3])
    xscr = nc.dram_tensor("xscr", [128, KT, TOK], F32, kind="Internal").ap()
    tscr_all = nc.dram_tensor("tscr", [depth, 128, NG * 128], BF16, kind="Internal").ap()
    bscr_all = nc.dram_tensor("bscr", [depth, 128, NG * 128], BF16, kind="Internal").ap()
    zscr = nc.dram_tensor("zscr", [depth, 128, 2 * NQ * 128], BF16, kind="Internal").ap()
    FS_LEN = 2 * NQ * NCMAX + 5 * NQ
    fscr = nc.dram_tensor("fscr", [depth, 128, FS_LEN], F32, kind="Internal").ap()

    stack = ExitStack()

    def sb(name, shape, dt):
        return stack.enter_context(nc.sbuf_tensor(name, list(shape), dt))

    xn = sb("xn", [128, KT, NTMAX], BF16)
    wring = sb("wring", [128, NSLOT, 2048], BF16)
    Zt = sb("Zt", [128, 2, NQ, 128], BF16)
    tb_ring = sb("tb_ring", [128, 2, 2, 8 * 128], BF16)
    ftab = sb("ftab", [128, 2 * NQ * NCMAX + 5 * NQ], F32)
    cs_t = ftab[:, 0:NQ * NCMAX].rearrange("p (q n) -> p q n", n=NCMAX)
    sn_t = ftab[:, NQ * NCMAX:2 * NQ * NCMAX].rearrange("p (q n) -> p q n", n=NCMAX)
    _fo = 2 * NQ * NCMAX
    a8t = ftab[:, _fo:_fo + 2 * NQ].rearrange("p (r q) -> p r q", r=2)
    a4t = ftab[:, _fo + 2 * NQ:_fo + 4 * NQ].rearrange("p (r q) -> p r q", r=2)
    rho8 = ftab[:, _fo + 4 * NQ:_fo + 5 * NQ]
    rstd_sb = sb("rstd_sb", [128, 2, NTMAX], F32)
    sm = sb("sm", [128, SM_LEN], F32)
    ada = sb("ada", [128, depth, 48, NCOLS_ADA], F32)
    bada_sb = sb("bada_sb", [128, depth, 48], F32)
    g1 = sb("g1", [128, KT, NCOLS_ADA], F32)
    rgw = sb("rgw", [128, 2 * NH * 128], BF16)
    ones_bf = sb("ones_bf", [128, 128], BF16)
    ident_bf = sb("ident_bf", [128, 128], BF16)
    cmask = sb("cmask", [128, 128], F32)
    selF = sb("selF", [128, 8, 240], BF16)
    selB = sb("selB", [128, 8, 240], BF16)
    kidx = sb("kidx", [128, NCMAX], F32)
    id2_bf = sb("id2_bf", [128, 64], BF16)
    cT_bf = sb("cT_bf", [128, KT, NCOLS_ADA], BF16)
    fg_sb = sb("fg_sb", [128, KT], F32)
    cH = sb("cH", [128, 2, NQ], F32)
    chh = sb("chh", [128, NH], F32)
    ccv = sb("ccv", [128, NH, 3], F32)
    phir = sb("phir", [128, NQ], F32)
    scrg = sb("scrg", [128, NH], F32)
    hbg = sb("hbg", [128, 8], F32)
    hbr = sb("hbr", [128, 4, NH], F32)
    sth_sb = sb("sth_sb", [128, NH, max(NS, 1)], F32)
    stcv_sb = sb("stcv_sb", [128, NH, max(NS, 1), 3], F32)
    hs_out = sb("hs_out", [128, NH, max(NS, 1)], F32)
    cvs_out = sb("cvs_out", [128, NH, max(NS, 1), 3], F32)
    s5s_out = sb("s5s_out", [128, 2, NQ, max(NS, 1)], F32)
    ARENA = 20480
    arena = sb("arena", [128, ARENA], F32)
    arena_bf = arena.bitcast(BF16)
    psb = [stack.enter_context(nc.psum_tensor("ps%d" % i, [128, 512], F32)) for i in range(8)]

    class Ar:
        def __init__(self, base):
            self.o = base

        def f32(self, n):
            o = (self.o + 3) // 4
            self.o = (o + n) * 4
            assert self.o <= ARENA * 4, "arena overflow %d" % self.o
            return arena[:, o:o + n]

        def bf(self, n):
            o = (self.o + 1) // 2
            self.o = (o + n) * 2
            assert self.o <= ARENA * 4, "arena overflow %d" % self.o
            return arena_bf[:, o:o + n]

    R0 = 0
    R1 = R0 + KT * NTMAX * 4
    R2 = R1 + 8 * NTMAX * 2
    R3 = R2 + 8 * NTMAX * 2
    xbuf = arena[:, 0:KT * NTMAX].rearrange("p (k t) -> p k t", t=NTMAX)
    yb = arena_bf[:, 0:KT * NTMAX].rearrange("p (k t) -> p k t", t=NTMAX)
    merged = arena_bf[:, KT * NTMAX:2 * KT * NTMAX].rearrange("p (k t) -> p k t", t=NTMAX)
    yg = arena_bf[:, R1 // 2:R1 // 2 + 8 * NTMAX].rearrange("p (k t) -> p k t", t=NTMAX)
    ya = arena_bf[:, R2 // 2:R2 // 2 + 8 * NTMAX].rearrange("p (k t) -> p k t", t=NTMAX)

    def isap(x):
        return not isinstance(x, (int, float)) and x is not None

    def mm(out, lhsT, rhs, start, stop):
        P.add("pe", lambda e, o=out, l=lhsT, r=rhs, s=start, t=stop: e.matmul(o, l, r, start=s, stop=t),
              [lhsT, rhs], [out])

    def act(out, in_, func, bias=None, scale=None):
        kw = {}
        rd = [in_]
        if bias is not None:
            kw["bias"] = bias
            if isap(bias):
                rd.append(bias)
        if scale is not None:
            kw["scale"] = scale
            if isap(scale):
                rd.append(scale)
        P.add("act", lambda e, o=out, i=in_, f=func, k=kw: e.activation(out=o, in_=i, func=f, **k), rd, [out])

    def tt(out, a, b, op, eng="dve"):
        P.add(eng, lambda e, o=out, x=a, y=b, p=op: e.tensor_tensor(out=o, in0=x, in1=y, op=p), [a, b], [out])

    def ts(out, a, s1, op0, s2=None, op1=None, eng="dve"):
        rd = [a] + [s for s in (s1, s2) if isap(s)]
        if op1 is None:
            P.add(eng, lambda e, o=out, x=a, q=s1, p=op0: e.tensor_scalar(out=o, in0=x, scalar1=q, scalar2=None,
                                                                          op0=p), rd, [out])
        else:
            P.add(eng, lambda e, o=out, x=a, q=s1, r=s2, p=op0, p1=op1: e.tensor_scalar(
                out=o, in0=x, scalar1=q, scalar2=r, op0=p, op1=p1), rd, [out])

    def stt(out, in0, scalar, in1, op0, op1, eng="dve"):
        rd = [in0, in1] + ([scalar] if isap(scalar) else [])
        P.add(eng, lambda e, o=out, x=in0, s=scalar, y=in1, p=op0, q=op1: e.scalar_tensor_tensor(
            out=o, in0=x, scalar=s, in1=y, op0=p, op1=q), rd, [out])

    def cp(out, in_, eng="dve"):
        if eng == "act":
            act(out, in_, AF.Copy)
        else:
            P.add(eng, lambda e, o=out, i=in_: e.tensor_copy(out=o, in_=i), [in_], [out])

    def recip(out, in_):
        P.add("dve", lambda e, o=out, i=in_: e.reciprocal(out=o, in_=i), [in_], [out])

    def scan(out, d0, d1, init):
        rd = [d0, d1] + ([init] if isap(init) else [])
        P.add("dve", lambda e, o=out, a=d0, b=d1, i=init: e.tensor_tensor_scan(
            out=o, data0=a, data1=b, initial=i, op0=ALU.mult, op1=ALU.add), rd, [out])

    def memset(ap, v, eng="dve"):
        P.add(eng, lambda e, a=ap, c=v: e.memset(a, c), [], [ap])

    def dma(q, out, in_, **kw):
        P.add(q, lambda e, o=out, i=in_, k=kw: e.dma_start(out=o, in_=i, **k), [in_], [out], dma=True)

    pstate = {"m": 0, "s": 0}

    def ps_main():
        b = psb[pstate["m"] % 5]
        pstate["m"] += 1
        return b

    def newps_dedicated(np_, ns):
        r = [psb[5][:, 0:np_]]
        if ns:
            r.append(psb[7][:, 0:4 * ns])
        return r

    def ps_samp():
        i = pstate["s"] % 8
        pstate["s"] += 1
        return psb[6][:, i * 64:(i + 1) * 64]

    class WStream:
        def __init__(self):
            self.seq = []
            self.issued = 0
            self.used = 0

        def plan(self, ap, K, save=None):
            self.seq.append((ap, K, save))

        def _issue(self):
            if self.issued < len(self.seq):
                ap, K, save = self.seq[self.issued]
                slot = wring[:, self.issued % NSLOT, 0:K * 128]
                dma("pool", slot, ap, max_dma_last_dim=8192)
                if save is not None:
                    dma("pool", save, slot)
                self.issued += 1

        def next(self, K):
            while self.issued < min(self.used + NSLOT, len(self.seq)):
                self._issue()
            ap, K2, _sv = self.seq[self.used]
            assert K2 == K, (K, K2, self.used)
            slot = wring[:, self.used % NSLOT, 0:K * 128].rearrange("p (k c) -> p k c", c=128)
            self.used += 1
            return slot

        def done(self):
            while self.issued < min(self.used + NSLOT, len(self.seq)):
                self._issue()

    W = WStream()
    for l in range(depth):
        for m in range(48):
            W.plan(wada[l, :, m * 2048:(m + 1) * 2048], 16)
    for l in range(depth):
        for ti in range(len(tiles)):
            for (kind, idx, K) in UNITS:
                off, _ = UNIT_OFF[(kind, idx)]
                W.plan(wst[l, :, off:off + K * 128], K)

    dma("pool", ones_bf[:], c_ones)
    dma("pool", ident_bf[:], c_ident)
    dma("pool", id2_bf[:], c_id2)
    dma("pool", selF[:].rearrange("p a b -> p (a b)"), c_selF)
    dma("pool", selB[:].rearrange("p a b -> p (a b)"), c_selB)
    dma("pool", cT_bf[:].rearrange("p a b -> p (a b)"), cT.rearrange("p a b -> p (a b)"))
    dma("sp", cmask[:], c_cmask)
    dma("sp", kidx[:], c_kidx)
    dma("sp", fg_sb[:], fgain)
    dma("sp", bada_sb[:].rearrange("p a b -> p (a b)"), bada.rearrange("p a b -> p (a b)"))

    def ada_phase(l):
        for m in range(48):
            slot = W.next(16)
            if m % 16 == 0:
                pbank = ps_main()
            po = pbank[:, (m % 16) * NCOLS_ADA:(m % 16 + 1) * NCOLS_ADA]
            for k in range(KT):
                mm(po, slot[:, k, :], cT_bf[:, k, :], k == 0, k == KT - 1)
            W.done()
            if m % 16 == 15:
                m0 = m - 15
                tt(ada[:, l, m0:m0 + 16, :],
                   pbank[:, 0:16 * NCOLS_ADA].rearrange("p (m c) -> p m c", c=NCOLS_ADA),
                   bada_sb[:, l, m0:m0 + 16].unsqueeze(2).to_broadcast([128, 16, NCOLS_ADA]), ALU.add)

    def smv(name, i0=None, i1=None):
        o, n = SM[name]
        if i0 is None:
            return sm[:, o:o + n]
        return sm[:, o + i0:o + (i1 if i1 is not None else i0 + 1)]

    def range_reduce(out, ang, tmp):
        ts(tmp, ang, 1.0 / TWO_PI, ALU.mult, MAGIC, ALU.add)
        ts(tmp, tmp, -MAGIC, ALU.add, -TWO_PI, ALU.mult)
        tt(out, ang, tmp, ALU.add)
        ts(out, out, float(np.pi), ALU.min, float(-np.pi), ALU.max)

    def cmul(or_, oi_, xr, xi, yr, yi, t1, t2, eng="dve"):
        tt(t1, xr, yr, ALU.mult, eng=eng)
        tt(t2, xi, yi, ALU.mult, eng=eng)
        tt(or_, t1, t2, ALU.subtract, eng=eng)
        tt(t1, xr, yi, ALU.mult, eng=eng)
        tt(t2, xi, yr, ALU.mult, eng=eng)
        tt(oi_, t1, t2, ALU.add, eng=eng)

    prep_state = {}

    def s5_prep(l, part):
        if part == "b":
            return prep_state.pop(l)()
        A = Ar(0)
        raw = A.f32(S5R_LEN)
        dma("sp", raw, s5rd[l])

        def rv(name, sh=None):
            o, n = S5R[name]
            v = raw[:, o:o + n]
            if sh:
                v = v.rearrange("p (q c) -> p q c", c=sh)
            return v
        lr_, li_, ld_ = rv("lam_re"), rv("lam_im"), rv("log_dt")
        bre, bim, cre, cim = rv("b_re", 16), rv("b_im", 16), rv("c_re", 16), rv("c_im", 16)
        v = [A.f32(NQ) for _ in range(16)]
        dt_, lrd, th, mag, sn_, cs_, ar, ai, t1, t2, t3, cfr, cfi, ir_, ii_, t4 = v
        act(dt_, ld_, AF.Exp)
        tt(lrd, lr_, dt_, ALU.mult)
        tt(th, li_, dt_, ALU.mult)
        act(mag, lrd, AF.Exp)
        act(rho8, lrd, AF.Exp, scale=8.0)
        range_reduce(t1, th, t2)
        act(sn_, t1, AF.Sin)
        ts(t3, th, float(np.pi / 2), ALU.add)
        range_reduce(t1, t3, t2)
        act(cs_, t1, AF.Sin)
        tt(ar, mag, cs_, ALU.mult)
        tt(ai, mag, sn_, ALU.mult)
        ts(t3, th, 8.0, ALU.mult)
        range_reduce(phir[:], t3, t2)
        ts(t1, ar, -1.0, ALU.add)
        tt(t2, lr_, lr_, ALU.mult)
        tt(t3, li_, li_, ALU.mult)
        tt(t2, t2, t3, ALU.add)
        recip(t2, t2)
        tt(t3, t1, lr_, ALU.mult)
        tt(t4, ai, li_, ALU.mult)
        tt(t3, t3, t4, ALU.add)
        tt(cfr, t3, t2, ALU.mult)
        tt(t3, ai, lr_, ALU.mult)
        tt(t4, t1, li_, ALU.mult)
        tt(t3, t3, t4, ALU.subtract)
        tt(cfi, t3, t2, ALU.mult)
        tt(t1, ar, ar, ALU.mult)
        tt(t2, ai, ai, ALU.mult)
        tt(t1, t1, t2, ALU.add)
        recip(t1, t1)
        tt(ir_, ar, t1, ALU.mult)
        tt(ii_, ai, t1, ALU.mult)
        ts(ii_, ii_, -1.0, ALU.mult)
        a2r, a2i = A.f32(NQ), A.f32(NQ)
        cmul(a2r, a2i, ar, ai, ar, ai, t1, t2)
        cmul(a4t[:, 0, :], a4t[:, 1, :], a2r, a2i, a2r, a2i, t1, t2)
        cmul(a8t[:, 0, :], a8t[:, 1, :], a4t[:, 0, :], a4t[:, 1, :], a4t[:, 0, :], a4t[:, 1, :], t1, t2)
        if cfg.get('stage', 99) < 2.2:
            return
        ang = A.f32(NQ * NCMAX).rearrange("p (q n) -> p q n", n=NCMAX)
        tm = A.f32(NQ * NCMAX).rearrange("p (q n) -> p q n", n=NCMAX)
        tm2 = A.f32(NQ * NCMAX).rearrange("p (q n) -> p q n", n=NCMAX)
        tt(ang, phir[:].unsqueeze(2).to_broadcast([128, NQ, NCMAX]),
           kidx[:].unsqueeze(1).to_broadcast([128, NQ, NCMAX]), ALU.mult)
        range_reduce(tm2, ang, tm)
        act(sn_t, tm2, AF.Sin)
        ts(ang, ang, float(np.pi / 2), ALU.add)
        range_reduce(tm2, ang, tm)
        act(cs_t, tm2, AF.Sin)
        if cfg.get('stage', 99) < 2.3:
            return
        A2 = Ar(A.o - 3 * NQ * NCMAX * 4)
        W3 = NQ * GC
        cur = [[A2.f32(W3).rearrange("p (q c) -> p q c", c=GC) for _ in range(2)] for _ in range(2)]
        curz = [[A2.f32(W3).rearrange("p (q c) -> p q c", c=GC) for _ in range(2)] for _ in range(2)]
        u1 = A2.f32(W3).rearrange("p (q c) -> p q c", c=GC)
        u2 = A2.f32(W3).rearrange("p (q c) -> p q c", c=GC)
        u1z = A2.f32(W3).rearrange("p (q c) -> p q c", c=GC)
        u2z = A2.f32(W3).rearrange("p (q c) -> p q c", c=GC)
        Xz = [[A2.bf(NQ * 128).rearrange("p (q f) -> p q f", f=128) for _ in range(2)] for _ in range(2)]
        for r in range(2):
            memset(Xz[0][r][64:128], 0.0)
            memset(Xz[1][r][0:64], 0.0)

        def bc(vv):
            return vv.unsqueeze(2).to_broadcast([128, NQ, GC])

        for i in range(7, -1, -1):
            if i == 7:
                zr, zi = cre, cim
            else:
                pr, pi_ = zr, zi
                zr, zi = curz[i % 2]
                cmul(zr, zi, pr, pi_, bc(ir_), bc(ii_), u1z, u2z, eng="pool")
            act(Zt[:, 0, :, i * 16:(i + 1) * 16], zr, AF.Copy)
            act(Zt[:, 1, :, i * 16:(i + 1) * 16], zi, AF.Copy)
        xr, xi = cur[1]
        cmul(xr, xi, bre, bim, bc(cfr), bc(cfi), u1, u2)
        for i in range(7, -1, -1):
            if i != 7:
                pr, pi_ = xr, xi
                xr, xi = cur[i % 2]
                cmul(xr, xi, pr, pi_, bc(ar), bc(ai), u1, u2)
            for g2 in range(2):
                hrows = slice(64 * g2, 64 * g2 + 64)
                act(Xz[g2][0][hrows, :, i * 16:(i + 1) * 16], xr[hrows], AF.Copy)
                act(Xz[g2][1][hrows, :, i * 16:(i + 1) * 16], xi[hrows], AF.Copy, scale=-1.0)
        prep_state[l] = lambda: prep_b(l, A2, Xz, rv)

    def prep_b(l, A2, Xz, rv):
        stg = [A2.bf(4 * 128 * 2) for _ in range(2)]
        tmpf = A2.f32(4 * 128)
        for gb in range(NG // 4):
            pT = ps_main()
            pB = ps_main()
            st = stg[gb % 2]
            for gi in range(4):
                g = gb * 4 + gi
                q, g2 = g // 2, g % 2
                mm(pT[:, gi * 128:(gi + 1) * 128], Xz[g2][0][:, q, :], Zt[:, 0, q, :], True, False)
                mm(pT[:, gi * 128:(gi + 1) * 128], Xz[g2][1][:, q, :], Zt[:, 1, q, :], False, True)
                mm(pB[:, gi * 128:gi * 128 + 64], Xz[g2][0][:, q, :], id2_bf[:], True, True)
                mm(pB[:, gi * 128 + 64:gi * 128 + 128], Xz[g2][1][:, q, :], id2_bf[:], True, True)
            dbg = cfg.get('dbg', 0)
            if dbg != 1:
                tt(tmpf.rearrange("p (g f) -> p g f", f=128), pT[:].rearrange("p (g f) -> p g f", f=128),
                   cmask[:].unsqueeze(1).to_broadcast([128, 4, 128]), ALU.mult)
            if dbg not in (1, 2):
                for gi in range(4):
                    g = gb * 4 + gi
                    stt(st[:, gi * 128:(gi + 1) * 128], ident_bf[:], rv("dcol")[:, g:g + 1],
                        tmpf[:, gi * 128:(gi + 1) * 128], ALU.mult, ALU.add)
            pBv = pB[:].rearrange("p (g r f) -> p g r f", r=2, f=64)
            stv = st[:, 512:1024].rearrange("p (g r f) -> p g r f", r=2, f=64)
            if dbg not in (1, 2, 3):
                act(stv[:, :, 0, :], pBv[:, :, 0, :], AF.Copy)
                act(stv[:, :, 1, :], pBv[:, :, 1, :], AF.Copy, scale=-1.0)
            dma("sp", tscr_all[l, :, gb * 512:(gb + 1) * 512], st[:, 0:512])
            dma("sp", bscr_all[l, :, gb * 512:(gb + 1) * 512], st[:, 512:1024])
        dma("sp", zscr[l], Zt[:].rearrange("p r q f -> p (r q f)"))
        dma("sp", fscr[l], ftab[:])

    def layer_setup(l):
        dma("sp", sm[:], smd[l])
        dma("pool", rgw[:], rgwd[l], max_dma_last_dim=8192)
        if NS:
            dma("sp", sth_sb[:].rearrange("p a b -> p (a b)"), st_h[:, l].rearrange("p a b -> p (a b)"))
            dma("sp", stcv_sb[:].rearrange("p a b c -> p (a b c)"), st_cv[:, l].rearrange("p a b c -> p (a b c)"))
        ts(g1[:], ada[:, l, 16:32, :], 1.0, ALU.add)
        tt(g1[:], g1[:], smv("gain").unsqueeze(2).to_broadcast([128, KT, NCOLS_ADA]), ALU.mult)
        act(scrg[:], smv("lam"), AF.Exp, scale=-1.0)
        act(scrg[:], scrg[:], AF.Ln, bias=1.0)
        ts(scrg[:], scrg[:], -8.0, ALU.mult)
        ts(hbg[:], smv("b_glu"), 0.5, ALU.mult)
        ts(hbr[:, 0, :], smv("b_r"), 0.5, ALU.mult)
        ts(hbr[:, 1, :], smv("b_i"), 0.5, ALU.mult)
        ts(hbr[:, 2, :], smv("b_in", 32, 48), 0.5, ALU.mult)
        ts(hbr[:, 3, :], scrg[:], 0.5, ALU.mult)
        memset(cH[:], 0.0)
        memset(chh[:], 0.0)
        memset(ccv[:], 0.0)
        dma("sp", Zt[:].rearrange("p r q f -> p (r q f)"), zscr[l])
        dma("sp", ftab[:], fscr[l])

    def tile_info(ti):
        p0, np_, ns = tiles[ti]
        NT = np_ + 4 * ns
        parts = [(0, np_)] + ([(np_, NT)] if ns else [])
        return p0, np_, ns, NT, parts

    def sview(ap2d):
        return ap2d.rearrange("p (s t) -> p s t", t=4)

    def newps(np_, ns):
        r = [ps_main()[:, 0:np_]]
        if ns:
            r.append(ps_samp()[:, 0:4 * ns])
        return r

    def proj(K, rhs_of_k, np_, ns, parts):
        slot = W.next(K)
        pt = newps(np_, ns)
        for pi_, (c0, c1) in enumerate(parts):
            for k in range(K):
                mm(pt[pi_], slot[:, k, :], rhs_of_k(k)[:, c0:c1], k == 0, k == K - 1)
        W.done()
        return pt

    def xsrc(l):
        return (xp if l == 0 else xscr[:, :, 0:PT]), (xs if l == 0 else xscr[:, :, PT:TOK])

    P1BASE = ARENA * 4 - (2 * NTMAX * 4 + 2 * NTMAX * 2)

    def p1_bufs():
        A = Ar(P1BASE)
        xk = [A.f32(NTMAX) for _ in range(2)]
        sq = [A.bf(NTMAX) for _ in range(2)]
        return xk, sq

    def norm_pass1(l, ti, par):
        p0, np_, ns, NT, parts = tile_info(ti)
        srcp, srcs = xsrc(l)
        xk, sq = p1_bufs()
        pt = newps_dedicated(np_, ns)

        def load(k):
            dma("sp", xk[k % 2][:, 0:np_], srcp[:, k, p0:p0 + np_])
            if ns:
                dma("sp", xk[k % 2][:, np_:NT], srcs[:, k, :])
        load(0)
        for k in range(KT):
            b = xk[k % 2]
            if k + 1 < KT:
                load(k + 1)
            act(sq[k % 2][:, 0:NT], b[:, 0:NT], AF.Square)
            for pi_, (c0, c1) in enumerate(parts):
                mm(pt[pi_], ones_bf[:], sq[k % 2][:, c0:c1], k == 0, k == KT - 1)
            if k == KT - 1:
                for pi_, (c0, c1) in enumerate(parts):
                    act(rstd_sb[:, par, c0:c1], pt[pi_], AF.Sqrt, scale=1.0 / D, bias=EPS)
                recip(rstd_sb[:, par, 0:NT], rstd_sb[:, par, 0:NT])
            yield

    def norm_pass2(l, ti, par):
        p0, np_, ns, NT, parts = tile_info(ti)
        srcp, srcs = xsrc(l)
        xk, sq = p1_bufs()

        def load(k):
            dma("sp", xk[k % 2][:, 0:np_], srcp[:, k, p0:p0 + np_])
            if ns:
                dma("sp", xk[k % 2][:, np_:NT], srcs[:, k, :])
        load(0)
        for k in range(KT):
            b = xk[k % 2]
            if k + 1 < KT:
                load(k + 1)
            tt(b[:, 0:NT], b[:, 0:NT], rstd_sb[:, par, 0:NT], ALU.mult)
            act(xn[:, k, 0:np_], b[:, 0:np_], AF.Identity, scale=g1[:, k, 0:1], bias=ada[:, l, k, 0:1])
            if ns:
                hv = sview(b[:, np_:NT])
                tt(hv, hv, g1[:, k, 1:1 + ns].unsqueeze(2).to_broadcast([128, ns, 4]), ALU.mult)
                tt(sview(xn[:, k, np_:NT]), hv, ada[:, l, k, 1:1 + ns].unsqueeze(2).to_broadcast([128, ns, 4]),
                   ALU.add)
            yield

    def run_all(gen):
        for _ in gen:
            pass

    def s5_phase(l, ti):
        p0, np_, ns, NT, parts = tile_info(ti)
        NCp = np_ // LCH
        last_tile = (ti == len(tiles) - 1)
        A = Ar(R3)
        NCOL = ns + NCp + ns
        NCA = ns + NCp
        NCS = NCp + ns
        u_bf = [A.bf(NTMAX) for _ in range(2)]
        U2 = [A.bf(8 * NCOL).rearrange("p (g n) -> p g n", n=NCOL) for _ in range(2)]
        Y2g = [A.bf(8 * NCA).rearrange("p (g n) -> p g n", n=NCA) for _ in range(2)]
        A0 = Ar(R0)
        Ssb = [[A0.f32(4 * NCS).rearrange("p (q n) -> p q n", n=NCS) for _ in range(2)] for _ in range(2)]
        Gin = [A.f32(4 * NCp).rearrange("p (q n) -> p q n", n=NCp) for _ in range(2)]
        Gs = [A.f32(4 * NCp).rearrange("p (q n) -> p q n", n=NCp) for _ in range(2)]
        Hs = [A.f32(4 * NCp).rearrange("p (q n) -> p q n", n=NCp) for _ in range(2)]
        w1 = A.f32(4 * NCp).rearrange("p (q n) -> p q n", n=NCp)
        w2 = A.f32(4 * NCp).rearrange("p (q n) -> p q n", n=NCp)
        Pz = [[A.bf(4 * NCA).rearrange("p (q n) -> p q n", n=NCA) for _ in range(2)] for _ in range(2)]
        for r in range(2):
            memset(Pz[0][r][64:128], 0.0)
            memset(Pz[1][r][0:64], 0.0)
        HR = [slice(0, 64), slice(64, 128)]
        if ns:
            h0 = A0.f32(2 * NQ * ns).rearrange("p (r q s) -> p r q s", r=2, s=ns)
            dma("sp", h0.rearrange("p r q s -> p (r q s)"), st_s5[:, l].rearrange("p r q s -> p (r q s)"))
            sw = [A0.f32(4 * ns).rearrange("p (q s) -> p q s", s=ns) for _ in range(4)]

        def tbv(j):
            tb = tb_ring[:, j % 2]
            return (tb, tb[:, 0, :].rearrange("p (g f) -> p g f", f=128),
                    tb[:, 1, :].rearrange("p (g r f) -> p g r f", r=2, f=64))

        def stageA(j):
            ub = u_bf[j % 2]
            tb, Tg, Bg = tbv(j)
            dma("sp", tb[:, 0, :], tscr_all[l, :, j * 1024:(j + 1) * 1024])
            dma("sp", tb[:, 1, :], bscr_all[l, :, j * 1024:(j + 1) * 1024])
            pt = proj(16, lambda k: xn[:, k, :], np_, ns, parts)
            for pi_, (c0, c1) in enumerate(parts):
                act(ub[:, c0:c1], pt[pi_], AF.Identity, bias=smv("b_in", j))
            u2 = U2[j % 2]
            gpb = 512 // NCOL
            for gl in range(8):
                if gl % gpb == 0:
                    pr = ps_main()
                    gl0 = gl
                po = pr[:, (gl - gl0) * NCOL:(gl - gl0 + 1) * NCOL]
                for i in range(LCH):
                    mm(po[:, ns:ns + NCp], selF[:, gl, (7 - i) * 16:(7 - i) * 16 + 128], ub[:, i:np_:LCH],
                       i == 0, i == LCH - 1)
                if ns:
                    for t in range(4):
                        mm(po[:, 0:ns], selF[:, gl, (7 - t) * 16:(7 - t) * 16 + 128], ub[:, np_ + t:NT:4],
                           t == 0, t == 3)
                    for t in range(4):
                        mm(po[:, ns + NCp:NCOL], selF[:, gl, (3 - t) * 16:(3 - t) * 16 + 128],
                           ub[:, np_ + t:NT:4], t == 0, t == 3)
                if gl == 7 or (gl + 1) % gpb == 0:
                    n = gl - gl0 + 1
                    act(u2[:, gl0:gl0 + n, :], pr[:, 0:n * NCOL].rearrange("p (g n) -> p g n", n=NCOL), AF.Copy)
            pS = [ps_main(), ps_main()]
            for gl in range(8):
                ql, g2 = gl // 2, gl % 2
                for r in range(2):
                    mm(pS[r][64 * g2:64 * g2 + 64, ql * NCS:(ql + 1) * NCS], Bg[:, gl, r, :],
                       u2[:, gl, ns:NCOL], True, True)
            for r in range(2):
                act(Ssb[j % 2][r], pS[r][:, 0:4 * NCS].rearrange("p (q n) -> p q n", n=NCS), AF.Copy)

        def stageB(j):
            q0 = 4 * j
            S_ = Ssb[j % 2]
            csv = cs_t[:, q0:q0 + 4, 0:NCp]
            snv = sn_t[:, q0:q0 + 4, 0:NCp]
            Sr, Si = S_[0][:, :, 0:NCp], S_[1][:, :, 0:NCp]
            tt(w1, csv, Sr, ALU.mult)
            tt(w2, snv, Si, ALU.mult)
            tt(Gin[0], w1, w2, ALU.add)
            tt(w1, csv, Si, ALU.mult)
            tt(w2, snv, Sr, ALU.mult)
            tt(Gin[1], w1, w2, ALU.subtract)
            for ql in range(4):
                q = q0 + ql
                for r in range(2):
                    scan(Gs[r][:, ql, :], rho8[:, q:q + 1].to_broadcast([128, NCp]), Gin[r][:, ql, :],
                         cH[:, r, q:q + 1])
            tt(w1, csv, Gs[0], ALU.mult)
            tt(w2, snv, Gs[1], ALU.mult)
            tt(Hs[0], w1, w2, ALU.subtract)
            tt(w1, csv, Gs[1], ALU.mult)
            tt(w2, snv, Gs[0], ALU.mult)
            tt(Hs[1], w1, w2, ALU.add)
            for g2 in range(2):
                hr = HR[g2]
                tt(Pz[g2][0][hr, :, ns:NCA], Hs[0][hr], Sr[hr], ALU.subtract)
                tt(Pz[g2][1][hr, :, ns:NCA], Si[hr], Hs[1][hr], ALU.subtract)
            for r in range(2):
                cp(cH[:, r, q0:q0 + 4], Hs[r][:, :, NCp - 1])
            if ns:
                h0r, h0i = h0[:, 0, q0:q0 + 4, :], h0[:, 1, q0:q0 + 4, :]

                def bq(tab, r):
                    return tab[:, r, q0:q0 + 4].unsqueeze(2).to_broadcast([128, 4, ns])
                tt(sw[0], bq(a8t, 0), h0r, ALU.mult)
                tt(sw[1], bq(a8t, 1), h0i, ALU.mult)
                for g2 in range(2):
                    tt(Pz[g2][0][HR[g2], :, 0:ns], sw[0][HR[g2]], sw[1][HR[g2]], ALU.subtract)
                tt(sw[0], bq(a8t, 0), h0i, ALU.mult)
                tt(sw[1], bq(a8t, 1), h0r, ALU.mult)
                tt(sw[2], sw[0], sw[1], ALU.add)
                for g2 in range(2):
                    ts(Pz[g2][1][HR[g2], :, 0:ns], sw[2][HR[g2]], -1.0, ALU.mult)
                tt(sw[0], bq(a4t, 0), h0r, ALU.mult)
                tt(sw[1], bq(a4t, 1), h0i, ALU.mult)
                tt(sw[2], sw[0], sw[1], ALU.subtract)
                tt(s5s_out[:, 0, q0:q0 + 4, :], sw[2], S_[0][:, :, NCp:NCS], ALU.add)
                tt(sw[0], bq(a4t, 0), h0i, ALU.mult)
                tt(sw[1], bq(a4t, 1), h0r, ALU.mult)
                tt(sw[2], sw[0], sw[1], ALU.add)
                tt(s5s_out[:, 1, q0:q0 + 4, :], sw[2], S_[1][:, :, NCp:NCS], ALU.add)

        def stageC(j):
            tb, Tg, Bg = tbv(j)
            u2 = U2[j % 2]
            y2 = Y2g[j % 2]
            gpa = 512 // NCA
            for gl in range(8):
                g = 8 * j + gl
                q, g2, ql = g // 2, g % 2, gl // 2
                if gl % gpa == 0:
                    pr = ps_main()
                    gl0 = gl
                po = pr[:, (gl - gl0) * NCA:(gl - gl0 + 1) * NCA]
                mm(po, Tg[:, gl, :], u2[:, gl, 0:NCA], True, False)
                mm(po, Zt[:, 0, q, :], Pz[g2][0][:, ql, :], False, False)
                mm(po, Zt[:, 1, q, :], Pz[g2][1][:, ql, :], False, True)
                if gl == 7 or (gl + 1) % gpa == 0:
                    n = gl - gl0 + 1
                    act(y2[:, gl0:gl0 + n, :], pr[:, 0:n * NCA].rearrange("p (g n) -> p g n", n=NCA), AF.Gelu)
            pt = newps(np_, ns)
            for i in range(LCH):
                for gl in range(8):
                    mm(pt[0][:, i:np_:LCH], selB[:, i, (7 - gl) * 16:(7 - gl) * 16 + 128], y2[:, gl, ns:NCA],
                       i == 0 and gl == 0, i == LCH - 1 and gl == 7)
            if ns:
                for t in range(4):
                    for gl in range(8):
                        mm(pt[1][:, t:4 * ns:4], selB[:, t, (7 - gl) * 16:(7 - gl) * 16 + 128], y2[:, gl, 0:ns],
                           t == 0 and gl == 0, t == 3 and gl == 7)
            for pi_, (c0, c1) in enumerate(parts):
                cp(yg[:, j, c0:c1], pt[pi_])

        stageA(0)
        for j in range(8):
            if j + 1 < 8:
                stageA(j + 1)
            stageB(j)
            stageC(j)
        if last_tile:
            dma("sp", o_s5p[:, l].rearrange("p r q -> p (r q)"), cH[:].rearrange("p r q -> p (r q)"))
        if ns:
            dma("sp", o_s5s[:, l].rearrange("p r q s -> p (r q s)"), s5s_out[:].rearrange("p r q s -> p (r q s)"))
        A = Ar(R3)
        sg = [A.f32(NTMAX) for _ in range(2)]
        sz = [A.f32(NTMAX) for _ in range(2)]
        tg = [A.f32(NTMAX) for _ in range(2)]
        for j in range(8):
            pt = proj(8, lambda k: yg[:, k, :], np_, ns, parts)
            for pi_, (c0, c1) in enumerate(parts):
                act(sg[j % 2][:, c0:c1], pt[pi_], AF.Tanh, bias=hbg[:, j:j + 1], scale=0.5)
            pt = proj(16, lambda k: xn[:, k, :], np_, ns, parts)
            for pi_, (c0, c1) in enumerate(parts):
                act(sz[j % 2][:, c0:c1], pt[pi_], AF.Silu, bias=smv("b_in", 8 + j))
            stt(tg[j % 2][:, 0:NT], sg[j % 2][:, 0:NT], 1.0, yg[:, j, 0:NT], ALU.add, ALU.mult)
            stt(ya[:, j, 0:NT], tg[j % 2][:, 0:NT], 0.5, sz[j % 2][:, 0:NT], ALU.mult, ALU.mult)

    def rg_phase(l, ti):
        p0, np_, ns, NT, parts = tile_info(ti)
        last_tile = (ti == len(tiles) - 1)
        A = Ar(R3)
        xcat = [A.f32(NTMAX + 4) for _ in range(2)]
        xcs = [A.f32(7 * max(ns, 1)).rearrange("p (s t) -> p s t", t=7) for _ in range(2)]
        cvl = [A.f32(NTMAX) for _ in range(2)]
        cvbl = [A.bf(NTMAX) for _ in range(2)]
        rr = A.f32(NTMAX)
        gi_ = A.f32(NTMAX)
        mq = A.f32(NTMAX)
        hh = A.f32(NTMAX)
        szb = A.f32(NTMAX)
        zz = A.f32(NTMAX)
        hprev = A.f32(max(ns, 1))
        rgwv = rgw[:].rearrange("p (r h c) -> p r h c", r=2, c=128)

        def head(h, pt):
            xc, cv, cvb, xs_ = xcat[h % 2], cvl[h % 2], cvbl[h % 2], xcs[h % 2]
            cp(xc[:, 0:3], ccv[:, h, :])
            act(xc[:, 3:3 + np_], pt[0], AF.Identity, bias=smv("b_in", 16 + h))
            cp(ccv[:, h, :], xc[:, np_:np_ + 3])
            if ns:
                cp(xs_[:, :, 0:3], stcv_sb[:, h, :, :])
                act(xs_[:, :, 3:7], sview(pt[1]), AF.Identity, bias=smv("b_in", 16 + h))
                cp(cvs_out[:, h, :, :], xs_[:, :, 4:7])
            cw = smv("conv_w", 4 * h, 4 * h + 4)
            act(cv[:, 0:np_], xc[:, 0:np_], AF.Identity, scale=cw[:, 0:1], bias=smv("conv_b", h))
            if ns:
                cvs = sview(cv[:, np_:NT])
                act(cvs, xs_[:, :, 0:4], AF.Identity, scale=cw[:, 0:1], bias=smv("conv_b", h))
            yield
            for k in range(1, 4):
                stt(cv[:, 0:np_], xc[:, k:k + np_], cw[:, k:k + 1], cv[:, 0:np_], ALU.mult, ALU.add)
            if ns:
                for k in range(1, 4):
                    stt(cvs, xs_[:, :, k:k + 4], cw[:, k:k + 1], cvs, ALU.mult, ALU.add)
            yield
            cp(cvb[:, 0:NT], cv[:, 0:NT], eng="act")
            yield

        def tail(h, ptr, pti, pzb):
            cv = cvl[h % 2]
            for pi_, (c0, c1) in enumerate(parts):
                act(rr[:, c0:c1], ptr[pi_], AF.Tanh, bias=hbr[:, 0, h:h + 1], scale=0.5)
                act(gi_[:, c0:c1], pti[pi_], AF.Tanh, bias=hbr[:, 1, h:h + 1], scale=0.5)
            act(rr[:, 0:NT], rr[:, 0:NT], AF.Exp, scale=hbr[:, 3, h:h + 1], bias=hbr[:, 3, h:h + 1])
            yield
            stt(mq[:, 0:NT], rr[:, 0:NT], -0.25, rr[:, 0:NT], ALU.mult, ALU.mult)
            ts(mq[:, 0:NT], mq[:, 0:NT], 0.25, ALU.add, 1e-30, ALU.max)
            stt(gi_[:, 0:NT], gi_[:, 0:NT], 1.0, cv[:, 0:NT], ALU.add, ALU.mult)
            for pi_, (c0, c1) in enumerate(parts):
                ts(zz[:, c0:c1], pzb[pi_], smv("b_in", 32 + h), ALU.add)
            yield
            act(mq[:, 0:NT], mq[:, 0:NT], AF.Sqrt)
            yield
            tt(mq[:, 0:NT], mq[:, 0:NT], gi_[:, 0:NT], ALU.mult)
            scan(hh[:, 0:np_], rr[:, 0:np_], mq[:, 0:np_], chh[:, h:h + 1])
            cp(chh[:, h:h + 1], hh[:, np_ - 1:np_])
            if ns:
                av, bv, hv = sview(rr[:, np_:NT]), sview(mq[:, np_:NT]), sview(hh[:, np_:NT])
                for t in range(4):
                    prev = sth_sb[:, h, :] if t == 0 else hv[:, :, t - 1]
                    tt(hprev[:, 0:ns], av[:, :, t], prev, ALU.mult)
                    tt(hv[:, :, t], hprev[:, 0:ns], bv[:, :, t], ALU.add)
                cp(hs_out[:, h, :], hv[:, :, 3])
            yield
            for pi_, (c0, c1) in enumerate(parts):
                act(szb[:, c0:c1], pzb[pi_], AF.Tanh, bias=hbr[:, 2, h:h + 1], scale=0.5)
            yield
            stt(zz[:, 0:NT], szb[:, 0:NT], 1.0, zz[:, 0:NT], ALU.add, ALU.mult)
            stt(yb[:, h, 0:NT], zz[:, 0:NT], 0.5, hh[:, 0:NT], ALU.mult, ALU.mult)
            yield

        def xbproj():
            return proj(16, lambda k: xn[:, k, :], np_, ns, parts)

        pxb = {0: xbproj(), 1: xbproj()}
        run_all(head(0, pxb.pop(0)))
        for h in range(NH):
            cvb = cvbl[h % 2]
            ptr = newps(np_, ns)
            pti = newps(np_, ns)
            for pi_, (c0, c1) in enumerate(parts):
                mm(ptr[pi_], rgwv[:, 0, h, :], cvb[:, c0:c1], True, True)
                mm(pti[pi_], rgwv[:, 1, h, :], cvb[:, c0:c1], True, True)
            pzb = proj(16, lambda k: xn[:, k, :], np_, ns, parts)
            T = tail(h, ptr, pti, pzb)
            H = head(h + 1, pxb.pop(h + 1)) if h + 1 < NH else iter(())
            next(T)
            next(H, None)
            if h + 2 < NH:
                pxb[h + 2] = xbproj()
            next(T)
            next(H, None)
            next(T)
            next(H, None)
            run_all(T)
        if last_tile:
            dma("sp", o_hp[:, l], chh[:])
            dma("sp", o_cvp[:, l].rearrange("p h t -> p (h t)"), ccv[:].rearrange("p h t -> p (h t)"))
        if ns:
            dma("sp", o_hs[:, l].rearrange("p h s -> p (h s)"), hs_out[:].rearrange("p h s -> p (h s)"))
            dma("sp", o_cvs[:, l].rearrange("p h s t -> p (h s t)"), cvs_out[:].rearrange("p h s t -> p (h s t)"))

    def merge_phase(l, ti, early=None):
        p0, np_, ns, NT, parts = tile_info(ti)
        A = Ar(R3)
        sga = [A.f32(NTMAX) for _ in range(2)]
        sgb = [A.f32(NTMAX) for _ in range(2)]
        t1 = A.f32(NTMAX)
        t2 = A.f32(NTMAX)
        assert A.o <= P1BASE
        for m in range(KT):
            pga = proj(16, lambda k: xn[:, k, :], np_, ns, parts)
            for pi_, (c0, c1) in enumerate(parts):
                act(sga[m % 2][:, c0:c1], pga[pi_], AF.Sigmoid, bias=smv("b_in", 48 + m))
            pgb = proj(16, lambda k: xn[:, k, :], np_, ns, parts)
            for pi_, (c0, c1) in enumerate(parts):
                act(sgb[m % 2][:, c0:c1], pgb[pi_], AF.Sigmoid, bias=smv("b_in", 64 + m))
            if early is not None:
                next(early, None)
            pa = proj(8, lambda k: ya[:, k, :], np_, ns, parts)
            for pi_, (c0, c1) in enumerate(parts):
                tt(t1[:, c0:c1], sga[m % 2][:, c0:c1], pa[pi_], ALU.mult)
            pb = proj(16, lambda k: yb[:, k, :], np_, ns, parts)
            for pi_, (c0, c1) in enumerate(parts):
                tt(t2[:, c0:c1], sgb[m % 2][:, c0:c1], pb[pi_], ALU.mult)
            tt(merged[:, m, 0:NT], t1[:, 0:NT], t2[:, 0:NT], ALU.add)

    def out_phase(l, ti, early=None):
        p0, np_, ns, NT, parts = tile_info(ti)
        srcp, srcs = xsrc(l)
        A = Ar(R3)
        NXR = 4
        xr = [A.f32(NTMAX) for _ in range(NXR)]
        xo = [A.f32(NTMAX) for _ in range(2)]
        tmo = A.f32(NTMAX)
        assert A.o <= P1BASE

        def reload(m):
            dma("sp", xr[m % NXR][:, 0:np_], srcp[:, m, p0:p0 + np_])
            if ns:
                dma("sp", xr[m % NXR][:, np_:NT], srcs[:, m, :])
        for m in range(min(NXR - 1, KT)):
            reload(m)
        for m in range(KT):
            if early is not None:
                next(early, None)
            if m + NXR - 1 < KT:
                reload(m + NXR - 1)
            po = proj(16, lambda k: merged[:, k, :], np_, ns, parts)
            r_, o_ = xr[m % NXR], xo[m % 2]
            stt(o_[:, 0:np_], po[0], ada[:, l, 32 + m, 0:1], r_[:, 0:np_], ALU.mult, ALU.add)
            if ns:
                tt(sview(tmo[:, np_:NT]), sview(po[1]),
                   ada[:, l, 32 + m, 1:1 + ns].unsqueeze(2).to_broadcast([128, ns, 4]), ALU.mult)
                tt(o_[:, np_:NT], tmo[:, np_:NT], r_[:, np_:NT], ALU.add)
            dma("sp", xscr[:, m, p0:p0 + np_], o_[:, 0:np_])
            if ns:
                dma("sp", xscr[:, m, PT:TOK], o_[:, np_:NT])

    def final_norm(ti):
        p0, np_, ns = tiles[ti]
        NT = np_ + 4 * ns
        parts = [(0, np_)] + ([(np_, NT)] if ns else [])
        HALF = KT * NTMAX * 4
        tmpsz = 2 * NTMAX * 2 + 3 * NTMAX * 4
        if 2 * HALF + 2 * tmpsz <= ARENA * 4:
            base = (ti % 2) * (HALF + tmpsz)
        else:
            base = 0
        A = Ar(base)
        xbuf = A.f32(KT * NTMAX).rearrange("p (k t) -> p k t", t=NTMAX)
        sq = [A.bf(NTMAX) for _ in range(2)]
        rstd = A.f32(NTMAX)
        yo = [A.f32(NTMAX) for _ in range(2)]
        dma("sp", xbuf[:, :, 0:np_], xscr[:, :, p0:p0 + np_])
        if ns:
            dma("sp", xbuf[:, :, np_:NT], xscr[:, :, PT:TOK])
        pt = [ps_main()[:, 0:np_]] + ([ps_samp()[:, 0:4 * ns]] if ns else [])
        for k in range(KT):
            act(sq[k % 2][:, 0:NT], xbuf[:, k, 0:NT], AF.Square)
            for pi_, (c0, c1) in enumerate(parts):
                mm(pt[pi_], ones_bf[:], sq[k % 2][:, c0:c1], k == 0, k == KT - 1)
        for pi_, (c0, c1) in enumerate(parts):
            act(rstd[:, c0:c1], pt[pi_], AF.Sqrt, scale=1.0 / D, bias=EPS)
        recip(rstd[:, 0:NT], rstd[:, 0:NT])
        for k in range(KT):
            o_ = yo[k % 2]
            stt(o_[:, 0:NT], xbuf[:, k, 0:NT], fg_sb[:, k:k + 1], rstd[:, 0:NT], ALU.mult, ALU.mult)
            dma("sp", yp[:, k, p0:p0 + np_], o_[:, 0:np_])
            if ns:
                dma("sp", ys[:, k, :], o_[:, np_:NT])

    for l in range(depth):
        s5_prep(l, part="a")
        ada_phase(l)
        s5_prep(l, part="b")
    assert len(tiles) >= 2
    seq = [(l, ti) for l in range(depth) for ti in range(len(tiles))]
    layer_setup(0)
    run_all(norm_pass1(0, 0, 0))
    run_all(norm_pass2(0, 0, 0))
    for idx, (l, ti) in enumerate(seq):
        nxt = seq[idx + 1] if idx + 1 < len(seq) else None
        par = (idx + 1) % 2
        s5_phase(l, ti)
        rg_phase(l, ti)
        if nxt is not None:
            g1_ = norm_pass1(nxt[0], nxt[1], par)
            merge_phase(l, ti, early=g1_)
            run_all(g1_)
            if nxt[0] != l:
                layer_setup(nxt[0])
            g2_ = norm_pass2(nxt[0], nxt[1], par)
            out_phase(l, ti, early=g2_)
            run_all(g2_)
        else:
            merge_phase(l, ti)
            out_phase(l, ti)
    for ti in range(len(tiles)):
        final_norm(ti)

    P.emit(stack)
    stack.close()
    return nc, P


FULL_CFG = {"depth": DEPTH, "P": 2048, "NS": 16,
            "tiles": [(0, 512, 0), (512, 512, 0), (1024, 512, 0), (1536, 512, 16)]}


def make_in_maps(inp, cfg, prompt_rows, sample_rows):
    depth, PT, NS = cfg["depth"], cfg["P"], cfg["NS"]
    ncmax = max(np_ // LCH for (_, np_, ns) in cfg["tiles"])
    consts = make_consts(ncmax)
    f = np.float32
    shared = {}
    shared["wst"] = np.stack([pack_layer_weights(inp["w_in"][l], inp["s5_w_glu"][l], inp["w_proj_a"][l],
                                                 inp["w_proj_b"][l], inp["w_out"][l]) for l in range(depth)])
    wa = np.asarray(inp["w_ada"][:depth], f).reshape(depth, 16, 128, 48, 128)
    shared["wada"] = np.ascontiguousarray(wa.transpose(0, 2, 3, 1, 4)).reshape(depth, 128, 48 * 2048)
    sml, s5l = zip(*[pack_small(inp, l) for l in range(depth)])
    shared["smd"] = np.stack(sml)
    shared["s5rd"] = np.stack(s5l)
    shared["bada"] = np.ascontiguousarray(
        np.asarray(inp["b_ada"][:depth], f).reshape(depth, 48, 128).transpose(2, 0, 1))
    shared["fgain"] = fm16(np.asarray(inp["final_gain"], f))
    rw = np.stack([np.asarray(inp["rg_w_r"][:depth], f), np.asarray(inp["rg_w_i"][:depth], f)], axis=1)
    shared["rgwd"] = np.ascontiguousarray(rw.transpose(0, 3, 1, 2, 4)).reshape(depth, 128, 2 * NH * 128)
    shared["c_ones"] = consts["ones"]
    shared["c_ident"] = consts["ident"]
    shared["c_cmask"] = consts["cmask"]
    shared["c_selF"] = consts["selF"].reshape(128, 8 * 240)
    shared["c_selB"] = consts["selB"].reshape(128, 8 * 240)
    shared["c_kidx"] = consts["kidx"]
    shared["c_id2"] = consts["id2"]
    maps = []
    for c in range(len(prompt_rows)):
        m = dict(shared)
        b = prompt_rows[c]
        srows = sample_rows[c]
        if b is not None:
            xpr = np.asarray(inp["x_prompt"][b][:PT], f)
            cpr = np.asarray(inp["c_prompt"][b], f)
        else:
            xpr = np.zeros((PT, D), f)
            cpr = np.zeros((D,), f)
        m["xp"] = np.ascontiguousarray(xpr.reshape(PT, KT, 128).transpose(2, 1, 0))
        xsr = np.asarray(inp["x_sample"][srows], f).reshape(NS * 4, KT, 128)
        m["xs"] = np.ascontiguousarray(xsr.transpose(2, 1, 0))
        call = np.concatenate([cpr[None, :], np.asarray(inp["c_sample"][srows], f)], axis=0)
        m["cT"] = np.ascontiguousarray(call.reshape(1 + NS, KT, 128).transpose(2, 1, 0))
        sre = np.asarray(inp["state_s5_re"][:depth][:, srows], f)
        sim = np.asarray(inp["state_s5_im"][:depth][:, srows], f)
        st = np.stack([sre, sim], axis=1)
        st = st.reshape(depth, 2, NS, NQ, 2, NP)
        m["st_s5"] = np.ascontiguousarray(st.transpose(4, 5, 0, 1, 3, 2)).reshape(128, depth, 2, NQ, NS)
        sh = np.asarray(inp["state_rglru_h"][:depth][:, srows], f).reshape(depth, NS, NH, 128)
        m["st_h"] = np.ascontiguousarray(sh.transpose(3, 0, 2, 1))
        sc = np.asarray(inp["state_conv"][:depth][:, srows], f).reshape(depth, NS, 3, NH, 128)
        m["st_cv"] = np.ascontiguousarray(sc.transpose(4, 0, 3, 1, 2))
        maps.append(m)
    return maps


def unpack_core(r, cfg):
    depth, PT, NS = cfg["depth"], cfg["P"], cfg["NS"]
    o = {}
    o["yp"] = r["yp"].transpose(2, 1, 0).reshape(PT, D)
    o["ys"] = r["ys"].transpose(2, 1, 0).reshape(NS, 4, D)
    s5p = r["o_s5p"].reshape(2, NP, depth, 2, NQ)
    s5p = s5p.transpose(2, 3, 4, 0, 1).reshape(depth, 2, NG, NP)
    o["s5p_re"], o["s5p_im"] = s5p[:, 0], s5p[:, 1]
    o["hp"] = r["o_hp"].transpose(1, 2, 0).reshape(depth, RGW)
    o["cvp"] = r["o_cvp"].transpose(1, 3, 2, 0).reshape(depth, 3, RGW)
    s5s = r["o_s5s"].reshape(2, NP, depth, 2, NQ, NS)
    s5s = s5s.transpose(2, 3, 5, 4, 0, 1).reshape(depth, 2, NS, NG, NP)
    o["s5s_re"], o["s5s_im"] = s5s[:, 0], s5s[:, 1]
    o["hs"] = r["o_hs"].transpose(1, 3, 2, 0).reshape(depth, NS, RGW)
    o["cvs"] = r["o_cvs"].transpose(1, 3, 4, 2, 0).reshape(depth, NS, 3, RGW)
    return o


_CACHE = {}


def kernel(**inputs):
    cfg = FULL_CFG
    inp = {k: np.asarray(v) for k, v in inputs.items()}
    B = inp["x_prompt"].shape[0]
    NS = cfg["NS"]
    prompt_rows = [c if c < B else None for c in range(N_CORES)]
    sample_rows = [list(range(c * NS, (c + 1) * NS)) for c in range(N_CORES)]
    maps = make_in_maps(inp, cfg, prompt_rows, sample_rows)
    if "nc" not in _CACHE:
        _CACHE["nc"] = build(cfg)[0]
    nc = _CACHE["nc"]
    res = run_bass_kernel_spmd(nc, maps, core_ids=list(range(N_CORES)))
    outs = [unpack_core(r, cfg) for r in res.results]
    f = np.float32
    y_prompt = np.stack([outs[b]["yp"] for b in range(B)]).astype(f)
    y_sample = np.concatenate([o["ys"] for o in outs], axis=0).astype(f)
    s5_re_p = np.stack([outs[b]["s5p_re"] for b in range(B)], axis=1).astype(f)
    s5_im_p = np.stack([outs[b]["s5p_im"] for b in range(B)], axis=1).astype(f)
    rg_h_p = np.stack([outs[b]["hp"] for b in range(B)], axis=1).astype(f)
    conv_p = np.stack([outs[b]["cvp"] for b in range(B)], axis=1).astype(f)
    s5_re_s = np.concatenate([o["s5s_re"] for o in outs], axis=1).astype(f)
    s5_im_s = np.concatenate([o["s5s_im"] for o in outs], axis=1).astype(f)
    rg_h_s = np.concatenate([o["hs"] for o in outs], axis=1).astype(f)
    conv_s = np.concatenate([o["cvs"] for o in outs], axis=1).astype(f)
    return (y_prompt, y_sample, s5_re_p, s5_im_p, rg_h_p, conv_p, s5_re_s, s5_im_s, rg_h_s, conv_s)
```

```python
import numpy as np
import concourse.bass as bass
import concourse.mybir as mybir
from concourse.bass_utils import run_bass_kernel_spmd

F32 = mybir.dt.float32
BF16 = mybir.dt.bfloat16
AF = mybir.ActivationFunctionType
ALU = mybir.AluOpType

D = 2048
KT = 16
DEPTH = 4
S5W = 1024
NG = 64
NP = 64
GC = 16
NQ = 32
RGW = 2048
NH = 16
INCOLS = 10240
LCH = 8
EPS = 1e-6
N_CORES = 8
TWO_PI = float(2 * np.pi)
MAGIC = 12582912.0


def _dsize(dt):
    return mybir.dt.size(dt)


def _box(ap):
    t = ap.tensor
    pat = ap.ap
    off = ap.offset
    esz = _dsize(ap.dtype)
    space = str(getattr(ap, "space", ""))
    name = t.name
    if "DRAM" in space.upper() or "Dram" in type(t).__name__ or "DRam" in type(t).__name__:
        ext = 0
        for st, cnt in pat:
            ext += abs(st) * (cnt - 1)
        return (name, 0, 1, off * esz, (off + ext + 1) * esz)
    row = pat[0][0]
    nparts = pat[0][1]
    if row == 0:
        row = 1 << 40
    p0 = off // row
    f0 = off % row
    ext = 0
    for st, cnt in pat[1:]:
        ext += abs(st) * (cnt - 1)
    f1 = f0 + ext + 1
    b0, b1 = f0 * esz, f1 * esz
    if "PSUM" in space.upper() or "PSum" in type(t).__name__:
        b0 = (b0 // 2048) * 2048
        b1 = ((b1 + 2047) // 2048) * 2048
    return (name, p0, p0 + nparts, b0, b1)


def _ovl(a, b):
    return a[1] < b[2] and b[1] < a[2] and a[3] < b[4] and b[3] < a[4]


def _covers(a, b):
    return a[1] <= b[1] and a[2] >= b[2] and a[3] <= b[3] and a[4] >= b[4]


class Prog:
    ENGS = ("pe", "act", "dve", "pool", "sp")
    KDMA = 8

    def __init__(self, nc):
        self.nc = nc
        self.ops = []
        self.eng_ops = {e: [] for e in self.ENGS}
        self.trk = {}

    def add(self, eng, fn, reads, writes, dma=False):
        oid = len(self.ops)
        deps = {}
        rb = [_box(a) for a in reads]
        wb = [_box(a) for a in writes]
        for b in rb:
            t = self.trk.setdefault(b[0], {"w": [], "r": {}})
            for (wbx, wop) in t["w"]:
                if _ovl(wbx, b):
                    deps[wop] = "RAW"
        for b in wb:
            t = self.trk.setdefault(b[0], {"w": [], "r": {}})
            for (wbx, wop) in t["w"]:
                if _ovl(wbx, b):
                    deps.setdefault(wop, "WAW")
            for (key, rop) in t["r"].items():
                if _ovl(key[1], b):
                    deps.setdefault(rop, "WAR")
        for b in rb:
            t = self.trk[b[0]]
            t["r"][(eng if not dma else ("dma", oid), b)] = oid
        for b in wb:
            t = self.trk[b[0]]
            t["w"] = [(x, o) for (x, o) in t["w"] if not _covers(b, x)]
            t["r"] = {k: o for (k, o) in t["r"].items() if not _covers(b, k[1])}
            t["w"].append((b, oid))
        deps.pop(oid, None)
        op = {"id": oid, "eng": eng, "fn": fn, "dma": dma, "deps": [], "signal": dma}
        for d, kind in deps.items():
            p = self.ops[d]
            if not p["dma"] and p["eng"] == eng:
                if eng == "pe":
                    continue
                if kind != "RAW" and not dma:
                    continue
                if dma and kind != "RAW":
                    pass
            op["deps"].append(d)
            p["signal"] = True
        self.ops.append(op)
        self.eng_ops[eng].append(op)
        return oid

    def emit(self, stack):
        nc = self.nc
        sems = {}
        for e in ("pe", "act", "dve", "pool"):
            sems[e] = stack.enter_context(nc.semaphore("s_" + e))
        dsem = {}
        for q in ("sp", "pool"):
            dsem[q] = [stack.enter_context(nc.semaphore("d_%s%d" % (q, i))) for i in range(self.KDMA)]
        cnt = {e: 0 for e in ("pe", "act", "dve", "pool")}
        dcnt = {"sp": 0, "pool": 0}
        final_dma = {}
        for op in self.ops:
            if op["dma"]:
                q = op["eng"]
                j = dcnt[q]
                dcnt[q] += 1
                s = dsem[q][j % self.KDMA]
                op["sig"] = (s, 16 * (j // self.KDMA + 1))
                op["pre"] = (s, 16 * (j // self.KDMA)) if j >= self.KDMA else None
                final_dma[(q, j % self.KDMA)] = op["sig"]
            elif op["signal"]:
                cnt[op["eng"]] += 1
                op["sig"] = (sems[op["eng"]], cnt[op["eng"]])
        handles = {"pe": "tensor", "act": "scalar", "dve": "vector", "pool": "gpsimd", "sp": "sync"}
        block = stack.enter_context(nc.Block())
        ops = self.ops
        eng_ops = self.eng_ops
        self.nwaits = 0
        prog = self

        def make(engname):
            def body(eng):
                waited = {}
                for op in eng_ops[engname]:
                    need = {}
                    if op["dma"] and op["pre"] is not None:
                        s, v = op["pre"]
                        need[id(s)] = (s, v)
                    for d in op["deps"]:
                        s, v = ops[d]["sig"]
                        if need.get(id(s), (None, 0))[1] < v:
                            need[id(s)] = (s, v)
                    for (s, v) in need.values():
                        if waited.get(id(s), 0) < v:
                            eng.wait_ge(s, v)
                            waited[id(s)] = v
                            prog.nwaits += 1
                    ins = op["fn"](eng)
                    if op["dma"]:
                        ins.then_inc(op["sig"][0], 16)
                    elif op["signal"]:
                        ins.then_inc(op["sig"][0], 1)
                if engname == "sp":
                    for (q, i), (s, v) in final_dma.items():
                        if waited.get(id(s), 0) < v:
                            eng.wait_ge(s, v)
            return body

        for e in self.ENGS:
            getattr(block, handles[e])(make(e))


def unit_list():
    u = []
    for j in range(8):
        u.append(("u", j, 16))
    for j in range(8):
        u.append(("glu", j, 8))
        u.append(("za", j, 16))
    u.append(("xb", 0, 16))
    u.append(("xb", 1, 16))
    for h in range(NH):
        u.append(("zb", h, 16))
        if h + 2 < NH:
            u.append(("xb", h + 2, 16))
    for m in range(KT):
        u.append(("ga", m, 16))
        u.append(("gb", m, 16))
        u.append(("wa", m, 8))
        u.append(("wb", m, 16))
    for m in range(KT):
        u.append(("wo", m, 16))
    return u


UNITS = unit_list()
UNIT_OFF = {}
_o = 0
for (_k, _i, _K) in UNITS:
    UNIT_OFF[(_k, _i)] = (_o, _K)
    _o += _K * 128
WSTREAM_LEN = _o

SM = {}
_o = 0
for _n, _l in [("b_in", 80), ("b_glu", 8), ("gain", 16), ("b_ada", 48), ("dcol", 64), ("conv_w", 64),
               ("conv_b", 16), ("b_r", 16), ("b_i", 16), ("lam", 16)]:
    SM[_n] = (_o, _l)
    _o += _l
SM_LEN = _o
S5R = {}
_o = 0
for _n, _l in [("lam_re", 32), ("lam_im", 32), ("log_dt", 32), ("b_re", 512), ("b_im", 512), ("c_re", 512),
               ("c_im", 512), ("dcol", 64)]:
    S5R[_n] = (_o, _l)
    _o += _l
S5R_LEN = _o


def make_consts(ncmax):
    c = {}
    c["ones"] = np.ones((128, 128), np.float32)
    c["ident"] = np.eye(128, dtype=np.float32)
    r = np.arange(128)
    c["cmask"] = ((r[None, :] // 16) >= (r[:, None] // 16)).astype(np.float32)
    bs = np.zeros((128, 8, 240), np.float32)
    for gl in range(8):
        for cc in range(16):
            bs[gl * 16 + cc, gl, 7 * 16 + cc] = 1.0
    c["selF"] = bs
    bb = np.zeros((128, 8, 240), np.float32)
    for ii in range(8):
        for cc in range(16):
            bb[ii * 16 + cc, ii, 7 * 16 + cc] = 1.0
    c["selB"] = bb
    c["id2"] = np.concatenate([np.eye(64, dtype=np.float32)] * 2, axis=0)
    c["kidx"] = np.tile(np.arange(1, ncmax + 1, dtype=np.float32)[None, :], (128, 1))
    return c


def pack_layer_weights(w_in, w_glu, w_pa, w_pb, w_out):
    out = np.empty((128, WSTREAM_LEN), np.float32)
    win = w_in.reshape(16, 128, INCOLS)
    wglu = w_glu.reshape(8, 128, S5W)
    wpa = w_pa.reshape(8, 128, D)
    wpb = w_pb.reshape(16, 128, D)
    wo = w_out.reshape(16, 128, D)
    colbase = {"u": 0, "za": 1024, "xb": 2048, "zb": 4096, "ga": 6144, "gb": 8192}
    for (kind, idx, K) in UNITS:
        off, _ = UNIT_OFF[(kind, idx)]
        if kind in colbase:
            c0 = colbase[kind] + idx * 128
            blk = win[:, :, c0:c0 + 128]
        elif kind == "glu":
            blk = wglu[:, :, idx * 128:(idx + 1) * 128]
        elif kind == "wa":
            blk = wpa[:, :, idx * 128:(idx + 1) * 128]
        elif kind == "wb":
            blk = wpb[:, :, idx * 128:(idx + 1) * 128]
        else:
            blk = wo[:, :, idx * 128:(idx + 1) * 128]
        out[:, off:off + K * 128] = blk.transpose(1, 0, 2).reshape(128, K * 128)
    return out


def fm16(v):
    return np.ascontiguousarray(v.reshape(-1, 128).T)


def gp_layout(a):
    sh = a.shape
    a = a.reshape(NQ, 2, NP, *sh[2:])
    a = np.moveaxis(a, 0, 2)
    return np.ascontiguousarray(a.reshape(128, NQ, *sh[2:]))


def pack_small(inp, l):
    sm = np.zeros((128, SM_LEN), np.float32)

    def put(name, arr):
        o, n = SM[name]
        sm[:, o:o + n] = arr.reshape(128, n)
    put("b_in", fm16(inp["b_in"][l]))
    put("b_glu", fm16(inp["s5_b_glu"][l]))
    put("gain", fm16(inp["norm_gain"][l]))
    put("b_ada", fm16(inp["b_ada"][l]))
    d = inp["s5_d"][l].reshape(NG, GC)
    dcol = np.broadcast_to(d.T[None, :, :], (LCH, GC, NG)).reshape(128, NG)
    put("dcol", dcol)
    cw = inp["rg_conv_w"][l]
    cwf = np.stack([fm16(cw[k]) for k in range(4)], axis=-1)
    put("conv_w", cwf)
    put("conv_b", fm16(inp["rg_conv_b"][l]))
    put("b_r", fm16(inp["rg_b_r"][l]))
    put("b_i", fm16(inp["rg_b_i"][l]))
    put("lam", fm16(inp["rg_lam"][l]))
    s5 = np.zeros((128, S5R_LEN), np.float32)

    def put5(name, arr):
        o, n = S5R[name]
        s5[:, o:o + n] = arr.reshape(128, n)
    put5("lam_re", gp_layout(inp["s5_lam_re"][l]))
    put5("lam_im", gp_layout(inp["s5_lam_im"][l]))
    put5("log_dt", gp_layout(np.broadcast_to(inp["s5_log_dt"][l][:, None], (NG, NP))))
    put5("b_re", gp_layout(inp["s5_b_re"][l]))
    put5("b_im", gp_layout(inp["s5_b_im"][l]))
    put5("c_re", gp_layout(inp["s5_c_re"][l].transpose(0, 2, 1)))
    put5("c_im", gp_layout(inp["s5_c_im"][l].transpose(0, 2, 1)))
    put5("dcol", dcol)
    return sm, s5


def build(cfg):
    from contextlib import ExitStack
    depth = cfg["depth"]
    PT = cfg["P"]
    NS = cfg["NS"]
    tiles = cfg["tiles"]
    NCOLS_ADA = 1 + NS
    NTMAX = max(np_ + 4 * ns for (_, np_, ns) in tiles)
    NCMAX = max(np_ // LCH for (_, np_, ns) in tiles)
    TOK = PT + 4 * NS
    NSLOT = 4

    nc = bass.Bass("TRN2", target_bir_lowering=False)
    P = Prog(nc)

    def din(name, shape):
        return nc.dram_tensor(name, list(shape), F32, kind="ExternalInput").ap()

    def dout(name, shape):
        return nc.dram_tensor(name, list(shape), F32, kind="ExternalOutput").ap()

    xp = din("xp", [128, KT, PT])
    xs = din("xs", [128, KT, 4 * NS])
    wst = din("wst", [depth, 128, WSTREAM_LEN])
    wada = din("wada", [depth, 128, 48 * 2048])
    cT = din("cT", [128, KT, NCOLS_ADA])
    bada = din("bada", [128, depth, 48])
    smd = din("smd", [depth, 128, SM_LEN])
    s5rd = din("s5rd", [depth, 128, S5R_LEN])
    fgain = din("fgain", [128, KT])
    rgwd = din("rgwd", [depth, 128, 2 * NH * 128])
    st_s5 = din("st_s5", [128, depth, 2, NQ, NS])
    st_h = din("st_h", [128, depth, NH, NS])
    st_cv = din("st_cv", [128, depth, NH, NS, 3])
    c_ones = din("c_ones", [128, 128])
    c_ident = din("c_ident", [128, 128])
    c_cmask = din("c_cmask", [128, 128])
    c_selF = din("c_selF", [128, 8 * 240])
    c_selB = din("c_selB", [128, 8 * 240])
    c_kidx = din("c_kidx", [128, NCMAX])
    c_id2 = din("c_id2", [128, 64])

    yp = dout("yp", [128, KT, PT])
    ys = dout("ys", [128, KT, 4 * NS])
    o_s5p = dout("o_s5p", [128, depth, 2, NQ])
    o_hp = dout("o_hp", [128, depth, NH])
    o_cvp = dout("o_cvp", [128, depth, NH, 3])
    o_s5s = dout("o_s5s", [128, depth, 2, NQ, NS])
    o_hs = dout("o_hs", [128, depth, NH, NS])
    o_cvs = dout("o_cvs", [128, depth, NH, NS, 3])
    xscr = nc.dram_tensor("xscr", [128, KT, TOK], F32, kind="Internal").ap()
    tscr_all = nc.dram_tensor("tscr", [depth, 128, NG * 128], BF16, kind="Internal").ap()
    bscr_all = nc.dram_tensor("bscr", [depth, 128, NG * 128], BF16, kind="Internal").ap()
    zscr = nc.dram_tensor("zscr", [depth, 128, 2 * NQ * 128], BF16, kind="Internal").ap()
    FS_LEN = 2 * NQ * NCMAX + 5 * NQ
    fscr = nc.dram_tensor("fscr", [depth, 128, FS_LEN], F32, kind="Internal").ap()

    stack = ExitStack()

    def sb(name, shape, dt):
        return stack.enter_context(nc.sbuf_tensor(name, list(shape), dt))

    xn = sb("xn", [128, KT, NTMAX], BF16)
    wring = sb("wring", [128, NSLOT, 2048], BF16)
    Zt = sb("Zt", [128, 2, NQ, 128], BF16)
    tb_ring = sb("tb_ring", [128, 2, 2, 8 * 128], BF16)
    ftab = sb("ftab", [128, 2 * NQ * NCMAX + 5 * NQ], F32)
    cs_t = ftab[:, 0:NQ * NCMAX].rearrange("p (q n) -> p q n", n=NCMAX)
    sn_t = ftab[:, NQ * NCMAX:2 * NQ * NCMAX].rearrange("p (q n) -> p q n", n=NCMAX)
    _fo = 2 * NQ * NCMAX
    a8t = ftab[:, _fo:_fo + 2 * NQ].rearrange("p (r q) -> p r q", r=2)
    a4t = ftab[:, _fo + 2 * NQ:_fo + 4 * NQ].rearrange("p (r q) -> p r q", r=2)
    rho8 = ftab[:, _fo + 4 * NQ:_fo + 5 * NQ]
    rstd_sb = sb("rstd_sb", [128, 2, NTMAX], F32)
    sm = sb("sm", [128, SM_LEN], F32)
    ada = sb("ada", [128, depth, 48, NCOLS_ADA], F32)
    bada_sb = sb("bada_sb", [128, depth, 48], F32)
    g1 = sb("g1", [128, KT, NCOLS_ADA], F32)
    rgw = sb("rgw", [128, 2 * NH * 128], BF16)
    ones_bf = sb("ones_bf", [128, 128], BF16)
    ident_bf = sb("ident_bf", [128, 128], BF16)
    cmask = sb("cmask", [128, 128], F32)
    selF = sb("selF", [128, 8, 240], BF16)
    selB = sb("selB", [128, 8, 240], BF16)
    kidx = sb("kidx", [128, NCMAX], F32)
    id2_bf = sb("id2_bf", [128, 64], BF16)
    cT_bf = sb("cT_bf", [128, KT, NCOLS_ADA], BF16)
    fg_sb = sb("fg_sb", [128, KT], F32)
    cH = sb("cH", [128, 2, NQ], F32)
    chh = sb("chh", [128, NH], F32)
    ccv = sb("ccv", [128, NH, 3], F32)
    phir = sb("phir", [128, NQ], F32)
    scrg = sb("scrg", [128, NH], F32)
    hbg = sb("hbg", [128, 8], F32)
    hbr = sb("hbr", [128, 4, NH], F32)
    sth_sb = sb("sth_sb", [128, NH, max(NS, 1)], F32)
    stcv_sb = sb("stcv_sb", [128, NH, max(NS, 1), 3], F32)
    hs_out = sb("hs_out", [128, NH, max(NS, 1)], F32)
    cvs_out = sb("cvs_out", [128, NH, max(NS, 1), 3], F32)
    s5s_out = sb("s5s_out", [128, 2, NQ, max(NS, 1)], F32)
    ARENA = 20480
    arena = sb("arena", [128, ARENA], F32)
    arena_bf = arena.bitcast(BF16)
    psb = [stack.enter_context(nc.psum_tensor("ps%d" % i, [128, 512], F32)) for i in range(8)]

    class Ar:
        def __init__(self, base):
            self.o = base

        def f32(self, n):
            o = (self.o + 3) // 4
            self.o = (o + n) * 4
            assert self.o <= ARENA * 4, "arena overflow %d" % self.o
            return arena[:, o:o + n]

        def bf(self, n):
            o = (self.o + 1) // 2
            self.o = (o + n) * 2
            assert self.o <= ARENA * 4, "arena overflow %d" % self.o
            return arena_bf[:, o:o + n]

    R0 = 0
    R1 = R0 + KT * NTMAX * 4
    R2 = R1 + 8 * NTMAX * 2
    R3 = R2 + 8 * NTMAX * 2
    xbuf = arena[:, 0:KT * NTMAX].rearrange("p (k t) -> p k t", t=NTMAX)
    yb = arena_bf[:, 0:KT * NTMAX].rearrange("p (k t) -> p k t", t=NTMAX)
    merged = arena_bf[:, KT * NTMAX:2 * KT * NTMAX].rearrange("p (k t) -> p k t", t=NTMAX)
    yg = arena_bf[:, R1 // 2:R1 // 2 + 8 * NTMAX].rearrange("p (k t) -> p k t", t=NTMAX)
    ya = arena_bf[:, R2 // 2:R2 // 2 + 8 * NTMAX].rearrange("p (k t) -> p k t", t=NTMAX)

    def isap(x):
        return not isinstance(x, (int, float)) and x is not None

    def mm(out, lhsT, rhs, start, stop):
        P.add("pe", lambda e, o=out, l=lhsT, r=rhs, s=start, t=stop: e.matmul(o, l, r, start=s, stop=t),
              [lhsT, rhs], [out])

    def act(out, in_, func, bias=None, scale=None):
        kw = {}
        rd = [in_]
        if bias is not None:
            kw["bias"] = bias
            if isap(bias):
                rd.append(bias)
        if scale is not None:
            kw["scale"] = scale
            if isap(scale):
                rd.append(scale)
        P.add("act", lambda e, o=out, i=in_, f=func, k=kw: e.activation(out=o, in_=i, func=f, **k), rd, [out])

    def tt(out, a, b, op, eng="dve"):
        P.add(eng, lambda e, o=out, x=a, y=b, p=op: e.tensor_tensor(out=o, in0=x, in1=y, op=p), [a, b], [out])

    def ts(out, a, s1, op0, s2=None, op1=None, eng="dve"):
        rd = [a] + [s for s in (s1, s2) if isap(s)]
        if op1 is None:
            P.add(eng, lambda e, o=out, x=a, q=s1, p=op0: e.tensor_scalar(out=o, in0=x, scalar1=q, scalar2=None,
                                                                          op0=p), rd, [out])
        else:
            P.add(eng, lambda e, o=out, x=a, q=s1, r=s2, p=op0, p1=op1: e.tensor_scalar(
                out=o, in0=x, scalar1=q, scalar2=r, op0=p, op1=p1), rd, [out])

    def stt(out, in0, scalar, in1, op0, op1, eng="dve"):
        rd = [in0, in1] + ([scalar] if isap(scalar) else [])
        P.add(eng, lambda e, o=out, x=in0, s=scalar, y=in1, p=op0, q=op1: e.scalar_tensor_tensor(
            out=o, in0=x, scalar=s, in1=y, op0=p, op1=q), rd, [out])

    def cp(out, in_, eng="dve"):
        if eng == "act":
            act(out, in_, AF.Copy)
        else:
            P.add(eng, lambda e, o=out, i=in_: e.tensor_copy(out=o, in_=i), [in_], [out])

    def recip(out, in_):
        P.add("dve", lambda e, o=out, i=in_: e.reciprocal(out=o, in_=i), [in_], [out])

    def scan(out, d0, d1, init):
        rd = [d0, d1] + ([init] if isap(init) else [])
        P.add("dve", lambda e, o=out, a=d0, b=d1, i=init: e.tensor_tensor_scan(
            out=o, data0=a, data1=b, initial=i, op0=ALU.mult, op1=ALU.add), rd, [out])

    def memset(ap, v, eng="dve"):
        P.add(eng, lambda e, a=ap, c=v: e.memset(a, c), [], [ap])

    def dma(q, out, in_, **kw):
        P.add(q, lambda e, o=out, i=in_, k=kw: e.dma_start(out=o, in_=i, **k), [in_], [out], dma=True)

    pstate = {"m": 0, "s": 0}

    def ps_main():
        b = psb[pstate["m"] % 5]
        pstate["m"] += 1
        return b

    def newps_dedicated(np_, ns):
        r = [psb[5][:, 0:np_]]
        if ns:
            r.append(psb[7][:, 0:4 * ns])
        return r

    def ps_samp():
        i = pstate["s"] % 8
        pstate["s"] += 1
        return psb[6][:, i * 64:(i + 1) * 64]

    class WStream:
        def __init__(self):
            self.seq = []
            self.issued = 0
            self.used = 0

        def plan(self, ap, K, save=None):
            self.seq.append((ap, K, save))

        def _issue(self):
            if self.issued < len(self.seq):
                ap, K, save = self.seq[self.issued]
                slot = wring[:, self.issued % NSLOT, 0:K * 128]
                dma("pool", slot, ap, max_dma_last_dim=8192)
                if save is not None:
                    dma("pool", save, slot)
                self.issued += 1

        def next(self, K):
            while self.issued < min(self.used + NSLOT, len(self.seq)):
                self._issue()
            ap, K2, _sv = self.seq[self.used]
            assert K2 == K, (K, K2, self.used)
            slot = wring[:, self.used % NSLOT, 0:K * 128].rearrange("p (k c) -> p k c", c=128)
            self.used += 1
            return slot

        def done(self):
            while self.issued < min(self.used + NSLOT, len(self.seq)):
                self._issue()

    W = WStream()
    for l in range(depth):
        for m in range(48):
            W.plan(wada[l, :, m * 2048:(m + 1) * 2048], 16)
    for l in range(depth):
        for ti in range(len(tiles)):
            for (kind, idx, K) in UNITS:
                off, _ = UNIT_OFF[(kind, idx)]
                W.plan(wst[l, :, off:off + K * 128], K)

    dma("pool", ones_bf[:], c_ones)
    dma("pool", ident_bf[:], c_ident)
    dma("pool", id2_bf[:], c_id2)
    dma("pool", selF[:].rearrange("p a b -> p (a b)"), c_selF)
    dma("pool", selB[:].rearrange("p a b -> p (a b)"), c_selB)
    dma("pool", cT_bf[:].rearrange("p a b -> p (a b)"), cT.rearrange("p a b -> p (a b)"))
    dma("sp", cmask[:], c_cmask)
    dma("sp", kidx[:], c_kidx)
    dma("sp", fg_sb[:], fgain)
    dma("sp", bada_sb[:].rearrange("p a b -> p (a b)"), bada.rearrange("p a b -> p (a b)"))

    def ada_phase(l):
        for m in range(48):
            slot = W.next(16)
            if m % 16 == 0:
                pbank = ps_main()
            po = pbank[:, (m % 16) * NCOLS_ADA:(m % 16 + 1) * NCOLS_ADA]
            for k in range(KT):
                mm(po, slot[:, k, :], cT_bf[:, k, :], k == 0, k == KT - 1)
            W.done()
            if m % 16 == 15:
                m0 = m - 15
                tt(ada[:, l, m0:m0 + 16, :],
                   pbank[:, 0:16 * NCOLS_ADA].rearrange("p (m c) -> p m c", c=NCOLS_ADA),
                   bada_sb[:, l, m0:m0 + 16].unsqueeze(2).to_broadcast([128, 16, NCOLS_ADA]), ALU.add)

    def smv(name, i0=None, i1=None):
        o, n = SM[name]
        if i0 is None:
            return sm[:, o:o + n]
        return sm[:, o + i0:o + (i1 if i1 is not None else i0 + 1)]

    def range_reduce(out, ang, tmp):
        ts(tmp, ang, 1.0 / TWO_PI, ALU.mult, MAGIC, ALU.add)
        ts(tmp, tmp, -MAGIC, ALU.add, -TWO_PI, ALU.mult)
        tt(out, ang, tmp, ALU.add)
        ts(out, out, float(np.pi), ALU.min, float(-np.pi), ALU.max)

    def cmul(or_, oi_, xr, xi, yr, yi, t1, t2):
        tt(t1, xr, yr, ALU.mult)
        tt(t2, xi, yi, ALU.mult)
        tt(or_, t1, t2, ALU.subtract)
        tt(t1, xr, yi, ALU.mult)
        tt(t2, xi, yr, ALU.mult)
        tt(oi_, t1, t2, ALU.add)

    prep_state = {}

    def s5_prep(l, part):
        if part == "b":
            return prep_state.pop(l)()
        A = Ar(0)
        raw = A.f32(S5R_LEN)
        dma("sp", raw, s5rd[l])

        def rv(name, sh=None):
            o, n = S5R[name]
            v = raw[:, o:o + n]
            if sh:
                v = v.rearrange("p (q c) -> p q c", c=sh)
            return v
        lr_, li_, ld_ = rv("lam_re"), rv("lam_im"), rv("log_dt")
        bre, bim, cre, cim = rv("b_re", 16), rv("b_im", 16), rv("c_re", 16), rv("c_im", 16)
        v = [A.f32(NQ) for _ in range(16)]
        dt_, lrd, th, mag, sn_, cs_, ar, ai, t1, t2, t3, cfr, cfi, ir_, ii_, t4 = v
        act(dt_, ld_, AF.Exp)
        tt(lrd, lr_, dt_, ALU.mult)
        tt(th, li_, dt_, ALU.mult)
        act(mag, lrd, AF.Exp)
        act(rho8, lrd, AF.Exp, scale=8.0)
        range_reduce(t1, th, t2)
        act(sn_, t1, AF.Sin)
        ts(t3, th, float(np.pi / 2), ALU.add)
        range_reduce(t1, t3, t2)
        act(cs_, t1, AF.Sin)
        tt(ar, mag, cs_, ALU.mult)
        tt(ai, mag, sn_, ALU.mult)
        ts(t3, th, 8.0, ALU.mult)
        range_reduce(phir[:], t3, t2)
        ts(t1, ar, -1.0, ALU.add)
        tt(t2, lr_, lr_, ALU.mult)
        tt(t3, li_, li_, ALU.mult)
        tt(t2, t2, t3, ALU.add)
        recip(t2, t2)
        tt(t3, t1, lr_, ALU.mult)
        tt(t4, ai, li_, ALU.mult)
        tt(t3, t3, t4, ALU.add)
        tt(cfr, t3, t2, ALU.mult)
        tt(t3, ai, lr_, ALU.mult)
        tt(t4, t1, li_, ALU.mult)
        tt(t3, t3, t4, ALU.subtract)
        tt(cfi, t3, t2, ALU.mult)
        tt(t1, ar, ar, ALU.mult)
        tt(t2, ai, ai, ALU.mult)
        tt(t1, t1, t2, ALU.add)
        recip(t1, t1)
        tt(ir_, ar, t1, ALU.mult)
        tt(ii_, ai, t1, ALU.mult)
        ts(ii_, ii_, -1.0, ALU.mult)
        a2r, a2i = A.f32(NQ), A.f32(NQ)
        cmul(a2r, a2i, ar, ai, ar, ai, t1, t2)
        cmul(a4t[:, 0, :], a4t[:, 1, :], a2r, a2i, a2r, a2i, t1, t2)
        cmul(a8t[:, 0, :], a8t[:, 1, :], a4t[:, 0, :], a4t[:, 1, :], a4t[:, 0, :], a4t[:, 1, :], t1, t2)
        if cfg.get('stage', 99) < 2.2:
            return
        ang = A.f32(NQ * NCMAX).rearrange("p (q n) -> p q n", n=NCMAX)
        tm = A.f32(NQ * NCMAX).rearrange("p (q n) -> p q n", n=NCMAX)
        tm2 = A.f32(NQ * NCMAX).rearrange("p (q n) -> p q n", n=NCMAX)
        tt(ang, phir[:].unsqueeze(2).to_broadcast([128, NQ, NCMAX]),
           kidx[:].unsqueeze(1).to_broadcast([128, NQ, NCMAX]), ALU.mult)
        range_reduce(tm2, ang, tm)
        act(sn_t, tm2, AF.Sin)
        ts(ang, ang, float(np.pi / 2), ALU.add)
        range_reduce(tm2, ang, tm)
        act(cs_t, tm2, AF.Sin)
        if cfg.get('stage', 99) < 2.3:
            return
        A2 = Ar(A.o - 3 * NQ * NCMAX * 4)
        W3 = NQ * GC
        cur = [[A2.f32(W3).rearrange("p (q c) -> p q c", c=GC) for _ in range(2)] for _ in range(2)]
        u1 = A2.f32(W3).rearrange("p (q c) -> p q c", c=GC)
        u2 = A2.f32(W3).rearrange("p (q c) -> p q c", c=GC)
        Xz = [[A2.bf(NQ * 128).rearrange("p (q f) -> p q f", f=128) for _ in range(2)] for _ in range(2)]
        for r in range(2):
            memset(Xz[0][r][64:128], 0.0)
            memset(Xz[1][r][0:64], 0.0)

        def bc(vv):
            return vv.unsqueeze(2).to_broadcast([128, NQ, GC])

        for i in range(7, -1, -1):
            if i == 7:
                zr, zi = cre, cim
            else:
                pr, pi_ = zr, zi
                zr, zi = cur[i % 2]
                cmul(zr, zi, pr, pi_, bc(ir_), bc(ii_), u1, u2)
            act(Zt[:, 0, :, i * 16:(i + 1) * 16], zr, AF.Copy)
            act(Zt[:, 1, :, i * 16:(i + 1) * 16], zi, AF.Copy)
        xr, xi = cur[1]
        cmul(xr, xi, bre, bim, bc(cfr), bc(cfi), u1, u2)
        for i in range(7, -1, -1):
            if i != 7:
                pr, pi_ = xr, xi
                xr, xi = cur[i % 2]
                cmul(xr, xi, pr, pi_, bc(ar), bc(ai), u1, u2)
            for g2 in range(2):
                hrows = slice(64 * g2, 64 * g2 + 64)
                act(Xz[g2][0][hrows, :, i * 16:(i + 1) * 16], xr[hrows], AF.Copy)
                act(Xz[g2][1][hrows, :, i * 16:(i + 1) * 16], xi[hrows], AF.Copy, scale=-1.0)
        prep_state[l] = lambda: prep_b(l, A2, Xz, rv)

    def prep_b(l, A2, Xz, rv):
        stg = [A2.bf(4 * 128 * 2) for _ in range(2)]
        tmpf = A2.f32(4 * 128)
        for gb in range(NG // 4):
            pT = ps_main()
            pB = ps_main()
            st = stg[gb % 2]
            for gi in range(4):
                g = gb * 4 + gi
                q, g2 = g // 2, g % 2
                mm(pT[:, gi * 128:(gi + 1) * 128], Xz[g2][0][:, q, :], Zt[:, 0, q, :], True, False)
                mm(pT[:, gi * 128:(gi + 1) * 128], Xz[g2][1][:, q, :], Zt[:, 1, q, :], False, True)
                mm(pB[:, gi * 128:gi * 128 + 64], Xz[g2][0][:, q, :], id2_bf[:], True, True)
                mm(pB[:, gi * 128 + 64:gi * 128 + 128], Xz[g2][1][:, q, :], id2_bf[:], True, True)
            dbg = cfg.get('dbg', 0)
            if dbg != 1:
                tt(tmpf.rearrange("p (g f) -> p g f", f=128), pT[:].rearrange("p (g f) -> p g f", f=128),
                   cmask[:].unsqueeze(1).to_broadcast([128, 4, 128]), ALU.mult)
            if dbg not in (1, 2):
                for gi in range(4):
                    g = gb * 4 + gi
                    stt(st[:, gi * 128:(gi + 1) * 128], ident_bf[:], rv("dcol")[:, g:g + 1],
                        tmpf[:, gi * 128:(gi + 1) * 128], ALU.mult, ALU.add)
            pBv = pB[:].rearrange("p (g r f) -> p g r f", r=2, f=64)
            stv = st[:, 512:1024].rearrange("p (g r f) -> p g r f", r=2, f=64)
            if dbg not in (1, 2, 3):
                act(stv[:, :, 0, :], pBv[:, :, 0, :], AF.Copy)
                act(stv[:, :, 1, :], pBv[:, :, 1, :], AF.Copy, scale=-1.0)
            dma("sp", tscr_all[l, :, gb * 512:(gb + 1) * 512], st[:, 0:512])
            dma("sp", bscr_all[l, :, gb * 512:(gb + 1) * 512], st[:, 512:1024])
        dma("sp", zscr[l], Zt[:].rearrange("p r q f -> p (r q f)"))
        dma("sp", fscr[l], ftab[:])

    def layer_setup(l):
        dma("sp", sm[:], smd[l])
        dma("pool", rgw[:], rgwd[l], max_dma_last_dim=8192)
        if NS:
            dma("sp", sth_sb[:].rearrange("p a b -> p (a b)"), st_h[:, l].rearrange("p a b -> p (a b)"))
            dma("sp", stcv_sb[:].rearrange("p a b c -> p (a b c)"), st_cv[:, l].rearrange("p a b c -> p (a b c)"))
        ts(g1[:], ada[:, l, 16:32, :], 1.0, ALU.add)
        tt(g1[:], g1[:], smv("gain").unsqueeze(2).to_broadcast([128, KT, NCOLS_ADA]), ALU.mult)
        act(scrg[:], smv("lam"), AF.Exp, scale=-1.0)
        act(scrg[:], scrg[:], AF.Ln, bias=1.0)
        ts(scrg[:], scrg[:], -8.0, ALU.mult)
        ts(hbg[:], smv("b_glu"), 0.5, ALU.mult)
        ts(hbr[:, 0, :], smv("b_r"), 0.5, ALU.mult)
        ts(hbr[:, 1, :], smv("b_i"), 0.5, ALU.mult)
        ts(hbr[:, 2, :], smv("b_in", 32, 48), 0.5, ALU.mult)
        ts(hbr[:, 3, :], scrg[:], 0.5, ALU.mult)
        memset(cH[:], 0.0)
        memset(chh[:], 0.0)
        memset(ccv[:], 0.0)
        dma("sp", Zt[:].rearrange("p r q f -> p (r q f)"), zscr[l])
        dma("sp", ftab[:], fscr[l])

    def tile_info(ti):
        p0, np_, ns = tiles[ti]
        NT = np_ + 4 * ns
        parts = [(0, np_)] + ([(np_, NT)] if ns else [])
        return p0, np_, ns, NT, parts

    def sview(ap2d):
        return ap2d.rearrange("p (s t) -> p s t", t=4)

    def newps(np_, ns):
        r = [ps_main()[:, 0:np_]]
        if ns:
            r.append(ps_samp()[:, 0:4 * ns])
        return r

    def proj(K, rhs_of_k, np_, ns, parts):
        slot = W.next(K)
        pt = newps(np_, ns)
        for pi_, (c0, c1) in enumerate(parts):
            for k in range(K):
                mm(pt[pi_], slot[:, k, :], rhs_of_k(k)[:, c0:c1], k == 0, k == K - 1)
        W.done()
        return pt

    def xsrc(l):
        return (xp if l == 0 else xscr[:, :, 0:PT]), (xs if l == 0 else xscr[:, :, PT:TOK])

    P1BASE = ARENA * 4 - (2 * NTMAX * 4 + 2 * NTMAX * 2)

    def p1_bufs():
        A = Ar(P1BASE)
        xk = [A.f32(NTMAX) for _ in range(2)]
        sq = [A.bf(NTMAX) for _ in range(2)]
        return xk, sq

    def norm_pass1(l, ti, par):
        p0, np_, ns, NT, parts = tile_info(ti)
        srcp, srcs = xsrc(l)
        xk, sq = p1_bufs()
        pt = newps_dedicated(np_, ns)

        def load(k):
            dma("sp", xk[k % 2][:, 0:np_], srcp[:, k, p0:p0 + np_])
            if ns:
                dma("sp", xk[k % 2][:, np_:NT], srcs[:, k, :])
        load(0)
        for k in range(KT):
            b = xk[k % 2]
            if k + 1 < KT:
                load(k + 1)
            act(sq[k % 2][:, 0:NT], b[:, 0:NT], AF.Square)
            for pi_, (c0, c1) in enumerate(parts):
                mm(pt[pi_], ones_bf[:], sq[k % 2][:, c0:c1], k == 0, k == KT - 1)
            if k == KT - 1:
                for pi_, (c0, c1) in enumerate(parts):
                    act(rstd_sb[:, par, c0:c1], pt[pi_], AF.Sqrt, scale=1.0 / D, bias=EPS)
                recip(rstd_sb[:, par, 0:NT], rstd_sb[:, par, 0:NT])
            yield

    def norm_pass2(l, ti, par):
        p0, np_, ns, NT, parts = tile_info(ti)
        srcp, srcs = xsrc(l)
        xk, sq = p1_bufs()

        def load(k):
            dma("sp", xk[k % 2][:, 0:np_], srcp[:, k, p0:p0 + np_])
            if ns:
                dma("sp", xk[k % 2][:, np_:NT], srcs[:, k, :])
        load(0)
        for k in range(KT):
            b = xk[k % 2]
            if k + 1 < KT:
                load(k + 1)
            tt(b[:, 0:NT], b[:, 0:NT], rstd_sb[:, par, 0:NT], ALU.mult)
            act(xn[:, k, 0:np_], b[:, 0:np_], AF.Identity, scale=g1[:, k, 0:1], bias=ada[:, l, k, 0:1])
            if ns:
                hv = sview(b[:, np_:NT])
                tt(hv, hv, g1[:, k, 1:1 + ns].unsqueeze(2).to_broadcast([128, ns, 4]), ALU.mult)
                tt(sview(xn[:, k, np_:NT]), hv, ada[:, l, k, 1:1 + ns].unsqueeze(2).to_broadcast([128, ns, 4]),
                   ALU.add)
            yield

    def run_all(gen):
        for _ in gen:
            pass

    def s5_phase(l, ti, extra=None):
        p0, np_, ns, NT, parts = tile_info(ti)
        NCp = np_ // LCH
        last_tile = (ti == len(tiles) - 1)
        A = Ar(R3)
        NCOL = ns + NCp + ns
        NCA = ns + NCp
        NCS = NCp + ns
        u_bf = [A.bf(NTMAX) for _ in range(2)]
        U2 = [A.bf(8 * NCOL).rearrange("p (g n) -> p g n", n=NCOL) for _ in range(2)]
        Y2g = [A.bf(8 * NCA).rearrange("p (g n) -> p g n", n=NCA) for _ in range(2)]
        A0 = Ar(R0)
        Ssb = [[A0.f32(4 * NCS).rearrange("p (q n) -> p q n", n=NCS) for _ in range(2)] for _ in range(2)]
        Gin = [A.f32(4 * NCp).rearrange("p (q n) -> p q n", n=NCp) for _ in range(2)]
        Gs = [A.f32(4 * NCp).rearrange("p (q n) -> p q n", n=NCp) for _ in range(2)]
        Hs = [A.f32(4 * NCp).rearrange("p (q n) -> p q n", n=NCp) for _ in range(2)]
        w1 = A.f32(4 * NCp).rearrange("p (q n) -> p q n", n=NCp)
        w2 = A.f32(4 * NCp).rearrange("p (q n) -> p q n", n=NCp)
        Pz = [[A.bf(4 * NCA).rearrange("p (q n) -> p q n", n=NCA) for _ in range(2)] for _ in range(2)]
        for r in range(2):
            memset(Pz[0][r][64:128], 0.0)
            memset(Pz[1][r][0:64], 0.0)
        HR = [slice(0, 64), slice(64, 128)]
        if ns:
            h0 = A0.f32(2 * NQ * ns).rearrange("p (r q s) -> p r q s", r=2, s=ns)
            dma("sp", h0.rearrange("p r q s -> p (r q s)"), st_s5[:, l].rearrange("p r q s -> p (r q s)"))
            sw = [A0.f32(4 * ns).rearrange("p (q s) -> p q s", s=ns) for _ in range(4)]

        assert A0.o <= R1 - 24 * NTMAX, "S5 scratch overlaps the streamed final-norm buffers"

        def tbv(j):
            tb = tb_ring[:, j % 2]
            return (tb, tb[:, 0, :].rearrange("p (g f) -> p g f", f=128),
                    tb[:, 1, :].rearrange("p (g r f) -> p g r f", r=2, f=64))

        def stageA(j):
            ub = u_bf[j % 2]
            tb, Tg, Bg = tbv(j)
            dma("sp", tb[:, 0, :], tscr_all[l, :, j * 1024:(j + 1) * 1024])
            dma("sp", tb[:, 1, :], bscr_all[l, :, j * 1024:(j + 1) * 1024])
            pt = proj(16, lambda k: xn[:, k, :], np_, ns, parts)
            for pi_, (c0, c1) in enumerate(parts):
                act(ub[:, c0:c1], pt[pi_], AF.Identity, bias=smv("b_in", j))
            u2 = U2[j % 2]
            gpb = 512 // NCOL
            for gl in range(8):
                if gl % gpb == 0:
                    pr = ps_main()
                    gl0 = gl
                po = pr[:, (gl - gl0) * NCOL:(gl - gl0 + 1) * NCOL]
                for i in range(LCH):
                    mm(po[:, ns:ns + NCp], selF[:, gl, (7 - i) * 16:(7 - i) * 16 + 128], ub[:, i:np_:LCH],
                       i == 0, i == LCH - 1)
                if ns:
                    for t in range(4):
                        mm(po[:, 0:ns], selF[:, gl, (7 - t) * 16:(7 - t) * 16 + 128], ub[:, np_ + t:NT:4],
                           t == 0, t == 3)
                    for t in range(4):
                        mm(po[:, ns + NCp:NCOL], selF[:, gl, (3 - t) * 16:(3 - t) * 16 + 128],
                           ub[:, np_ + t:NT:4], t == 0, t == 3)
                if gl == 7 or (gl + 1) % gpb == 0:
                    n = gl - gl0 + 1
                    act(u2[:, gl0:gl0 + n, :], pr[:, 0:n * NCOL].rearrange("p (g n) -> p g n", n=NCOL), AF.Copy)
            pS = [ps_main(), ps_main()]
            for gl in range(8):
                ql, g2 = gl // 2, gl % 2
                for r in range(2):
                    mm(pS[r][64 * g2:64 * g2 + 64, ql * NCS:(ql + 1) * NCS], Bg[:, gl, r, :],
                       u2[:, gl, ns:NCOL], True, True)
            for r in range(2):
                act(Ssb[j % 2][r], pS[r][:, 0:4 * NCS].rearrange("p (q n) -> p q n", n=NCS), AF.Copy)

        def stageB(j):
            q0 = 4 * j
            S_ = Ssb[j % 2]
            csv = cs_t[:, q0:q0 + 4, 0:NCp]
            snv = sn_t[:, q0:q0 + 4, 0:NCp]
            Sr, Si = S_[0][:, :, 0:NCp], S_[1][:, :, 0:NCp]
            tt(w1, csv, Sr, ALU.mult)
            tt(w2, snv, Si, ALU.mult)
            tt(Gin[0], w1, w2, ALU.add)
            tt(w1, csv, Si, ALU.mult)
            tt(w2, snv, Sr, ALU.mult)
            tt(Gin[1], w1, w2, ALU.subtract)
            for ql in range(4):
                q = q0 + ql
                for r in range(2):
                    scan(Gs[r][:, ql, :], rho8[:, q:q + 1].to_broadcast([128, NCp]), Gin[r][:, ql, :],
                         cH[:, r, q:q + 1])
            tt(w1, csv, Gs[0], ALU.mult)
            tt(w2, snv, Gs[1], ALU.mult)
            tt(Hs[0], w1, w2, ALU.subtract)
            tt(w1, csv, Gs[1], ALU.mult)
            tt(w2, snv, Gs[0], ALU.mult)
            tt(Hs[1], w1, w2, ALU.add)
            for g2 in range(2):
                hr = HR[g2]
                tt(Pz[g2][0][hr, :, ns:NCA], Hs[0][hr], Sr[hr], ALU.subtract)
                tt(Pz[g2][1][hr, :, ns:NCA], Si[hr], Hs[1][hr], ALU.subtract)
            for r in range(2):
                cp(cH[:, r, q0:q0 + 4], Hs[r][:, :, NCp - 1])
            if ns:
                h0r, h0i = h0[:, 0, q0:q0 + 4, :], h0[:, 1, q0:q0 + 4, :]

                def bq(tab, r):
                    return tab[:, r, q0:q0 + 4].unsqueeze(2).to_broadcast([128, 4, ns])
                tt(sw[0], bq(a8t, 0), h0r, ALU.mult)
                tt(sw[1], bq(a8t, 1), h0i, ALU.mult)
                for g2 in range(2):
                    tt(Pz[g2][0][HR[g2], :, 0:ns], sw[0][HR[g2]], sw[1][HR[g2]], ALU.subtract)
                tt(sw[0], bq(a8t, 0), h0i, ALU.mult)
                tt(sw[1], bq(a8t, 1), h0r, ALU.mult)
                tt(sw[2], sw[0], sw[1], ALU.add)
                for g2 in range(2):
                    ts(Pz[g2][1][HR[g2], :, 0:ns], sw[2][HR[g2]], -1.0, ALU.mult)
                tt(sw[0], bq(a4t, 0), h0r, ALU.mult)
                tt(sw[1], bq(a4t, 1), h0i, ALU.mult)
                tt(sw[2], sw[0], sw[1], ALU.subtract)
                tt(s5s_out[:, 0, q0:q0 + 4, :], sw[2], S_[0][:, :, NCp:NCS], ALU.add)
                tt(sw[0], bq(a4t, 0), h0i, ALU.mult)
                tt(sw[1], bq(a4t, 1), h0r, ALU.mult)
                tt(sw[2], sw[0], sw[1], ALU.add)
                tt(s5s_out[:, 1, q0:q0 + 4, :], sw[2], S_[1][:, :, NCp:NCS], ALU.add)

        def stageC(j):
            tb, Tg, Bg = tbv(j)
            u2 = U2[j % 2]
            y2 = Y2g[j % 2]
            gpa = 512 // NCA
            for gl in range(8):
                g = 8 * j + gl
                q, g2, ql = g // 2, g % 2, gl // 2
                if gl % gpa == 0:
                    pr = ps_main()
                    gl0 = gl
                po = pr[:, (gl - gl0) * NCA:(gl - gl0 + 1) * NCA]
                mm(po, Tg[:, gl, :], u2[:, gl, 0:NCA], True, False)
                mm(po, Zt[:, 0, q, :], Pz[g2][0][:, ql, :], False, False)
                mm(po, Zt[:, 1, q, :], Pz[g2][1][:, ql, :], False, True)
                if gl == 7 or (gl + 1) % gpa == 0:
                    n = gl - gl0 + 1
                    act(y2[:, gl0:gl0 + n, :], pr[:, 0:n * NCA].rearrange("p (g n) -> p g n", n=NCA), AF.Gelu)
            pt = newps(np_, ns)
            for i in range(LCH):
                for gl in range(8):
                    mm(pt[0][:, i:np_:LCH], selB[:, i, (7 - gl) * 16:(7 - gl) * 16 + 128], y2[:, gl, ns:NCA],
                       i == 0 and gl == 0, i == LCH - 1 and gl == 7)
            if ns:
                for t in range(4):
                    for gl in range(8):
                        mm(pt[1][:, t:4 * ns:4], selB[:, t, (7 - gl) * 16:(7 - gl) * 16 + 128], y2[:, gl, 0:ns],
                           t == 0 and gl == 0, t == 3 and gl == 7)
            for pi_, (c0, c1) in enumerate(parts):
                cp(yg[:, j, c0:c1], pt[pi_])

        stageA(0)
        for j in range(8):
            if j + 1 < 8:
                stageA(j + 1)
            if extra is not None:
                next(extra, None)
                next(extra, None)
            stageB(j)
            if extra is not None:
                next(extra, None)
                next(extra, None)
            stageC(j)
        if extra is not None:
            run_all(extra)
        if last_tile:
            dma("sp", o_s5p[:, l].rearrange("p r q -> p (r q)"), cH[:].rearrange("p r q -> p (r q)"))
        if ns:
            dma("sp", o_s5s[:, l].rearrange("p r q s -> p (r q s)"), s5s_out[:].rearrange("p r q s -> p (r q s)"))
        A = Ar(R3)
        sg = [A.f32(NTMAX) for _ in range(2)]
        sz = [A.f32(NTMAX) for _ in range(2)]
        tg = [A.f32(NTMAX) for _ in range(2)]
        for j in range(8):
            pt = proj(8, lambda k: yg[:, k, :], np_, ns, parts)
            for pi_, (c0, c1) in enumerate(parts):
                act(sg[j % 2][:, c0:c1], pt[pi_], AF.Tanh, bias=hbg[:, j:j + 1], scale=0.5)
            pt = proj(16, lambda k: xn[:, k, :], np_, ns, parts)
            for pi_, (c0, c1) in enumerate(parts):
                act(sz[j % 2][:, c0:c1], pt[pi_], AF.Silu, bias=smv("b_in", 8 + j))
            stt(tg[j % 2][:, 0:NT], sg[j % 2][:, 0:NT], 1.0, yg[:, j, 0:NT], ALU.add, ALU.mult)
            stt(ya[:, j, 0:NT], tg[j % 2][:, 0:NT], 0.5, sz[j % 2][:, 0:NT], ALU.mult, ALU.mult)

    def rg_phase(l, ti):
        p0, np_, ns, NT, parts = tile_info(ti)
        last_tile = (ti == len(tiles) - 1)
        A = Ar(R3)
        xcat = [A.f32(NTMAX + 4) for _ in range(2)]
        xcs = [A.f32(7 * max(ns, 1)).rearrange("p (s t) -> p s t", t=7) for _ in range(2)]
        cvl = [A.f32(NTMAX) for _ in range(2)]
        cvbl = [A.bf(NTMAX) for _ in range(2)]
        rr = A.f32(NTMAX)
        gi_ = A.f32(NTMAX)
        mq = A.f32(NTMAX)
        hh = A.f32(NTMAX)
        szb = A.f32(NTMAX)
        zz = A.f32(NTMAX)
        hprev = A.f32(max(ns, 1))
        rgwv = rgw[:].rearrange("p (r h c) -> p r h c", r=2, c=128)

        def head(h, pt):
            xc, cv, cvb, xs_ = xcat[h % 2], cvl[h % 2], cvbl[h % 2], xcs[h % 2]
            cp(xc[:, 0:3], ccv[:, h, :])
            act(xc[:, 3:3 + np_], pt[0], AF.Identity, bias=smv("b_in", 16 + h))
            cp(ccv[:, h, :], xc[:, np_:np_ + 3])
            if ns:
                cp(xs_[:, :, 0:3], stcv_sb[:, h, :, :])
                act(xs_[:, :, 3:7], sview(pt[1]), AF.Identity, bias=smv("b_in", 16 + h))
                cp(cvs_out[:, h, :, :], xs_[:, :, 4:7])
            cw = smv("conv_w", 4 * h, 4 * h + 4)
            act(cv[:, 0:np_], xc[:, 0:np_], AF.Identity, scale=cw[:, 0:1], bias=smv("conv_b", h))
            if ns:
                cvs = sview(cv[:, np_:NT])
                act(cvs, xs_[:, :, 0:4], AF.Identity, scale=cw[:, 0:1], bias=smv("conv_b", h))
            yield
            for k in range(1, 4):
                stt(cv[:, 0:np_], xc[:, k:k + np_], cw[:, k:k + 1], cv[:, 0:np_], ALU.mult, ALU.add)
            if ns:
                for k in range(1, 4):
                    stt(cvs, xs_[:, :, k:k + 4], cw[:, k:k + 1], cvs, ALU.mult, ALU.add)
            yield
            cp(cvb[:, 0:NT], cv[:, 0:NT], eng="act")
            yield

        def tail(h, ptr, pti, pzb):
            cv = cvl[h % 2]
            for pi_, (c0, c1) in enumerate(parts):
                act(rr[:, c0:c1], ptr[pi_], AF.Tanh, bias=hbr[:, 0, h:h + 1], scale=0.5)
                act(gi_[:, c0:c1], pti[pi_], AF.Tanh, bias=hbr[:, 1, h:h + 1], scale=0.5)
            act(rr[:, 0:NT], rr[:, 0:NT], AF.Exp, scale=hbr[:, 3, h:h + 1], bias=hbr[:, 3, h:h + 1])
            yield
            stt(mq[:, 0:NT], rr[:, 0:NT], -0.25, rr[:, 0:NT], ALU.mult, ALU.mult)
            ts(mq[:, 0:NT], mq[:, 0:NT], 0.25, ALU.add, 1e-30, ALU.max)
            stt(gi_[:, 0:NT], gi_[:, 0:NT], 1.0, cv[:, 0:NT], ALU.add, ALU.mult)
            for pi_, (c0, c1) in enumerate(parts):
                ts(zz[:, c0:c1], pzb[pi_], smv("b_in", 32 + h), ALU.add)
            yield
            act(mq[:, 0:NT], mq[:, 0:NT], AF.Sqrt)
            yield
            tt(mq[:, 0:NT], mq[:, 0:NT], gi_[:, 0:NT], ALU.mult)
            scan(hh[:, 0:np_], rr[:, 0:np_], mq[:, 0:np_], chh[:, h:h + 1])
            cp(chh[:, h:h + 1], hh[:, np_ - 1:np_])
            if ns:
                av, bv, hv = sview(rr[:, np_:NT]), sview(mq[:, np_:NT]), sview(hh[:, np_:NT])
                for t in range(4):
                    prev = sth_sb[:, h, :] if t == 0 else hv[:, :, t - 1]
                    tt(hprev[:, 0:ns], av[:, :, t], prev, ALU.mult)
                    tt(hv[:, :, t], hprev[:, 0:ns], bv[:, :, t], ALU.add)
                cp(hs_out[:, h, :], hv[:, :, 3])
            yield
            for pi_, (c0, c1) in enumerate(parts):
                act(szb[:, c0:c1], pzb[pi_], AF.Tanh, bias=hbr[:, 2, h:h + 1], scale=0.5)
            yield
            stt(zz[:, 0:NT], szb[:, 0:NT], 1.0, zz[:, 0:NT], ALU.add, ALU.mult)
            stt(yb[:, h, 0:NT], zz[:, 0:NT], 0.5, hh[:, 0:NT], ALU.mult, ALU.mult)
            yield

        def xbproj():
            return proj(16, lambda k: xn[:, k, :], np_, ns, parts)

        pxb = {0: xbproj(), 1: xbproj()}
        run_all(head(0, pxb.pop(0)))
        for h in range(NH):
            cvb = cvbl[h % 2]
            ptr = newps(np_, ns)
            pti = newps(np_, ns)
            for pi_, (c0, c1) in enumerate(parts):
                mm(ptr[pi_], rgwv[:, 0, h, :], cvb[:, c0:c1], True, True)
                mm(pti[pi_], rgwv[:, 1, h, :], cvb[:, c0:c1], True, True)
            pzb = proj(16, lambda k: xn[:, k, :], np_, ns, parts)
            T = tail(h, ptr, pti, pzb)
            H = head(h + 1, pxb.pop(h + 1)) if h + 1 < NH else iter(())
            next(T)
            next(H, None)
            if h + 2 < NH:
                pxb[h + 2] = xbproj()
            next(T)
            next(H, None)
            next(T)
            next(H, None)
            run_all(T)
        if last_tile:
            dma("sp", o_hp[:, l], chh[:])
            dma("sp", o_cvp[:, l].rearrange("p h t -> p (h t)"), ccv[:].rearrange("p h t -> p (h t)"))
        if ns:
            dma("sp", o_hs[:, l].rearrange("p h s -> p (h s)"), hs_out[:].rearrange("p h s -> p (h s)"))
            dma("sp", o_cvs[:, l].rearrange("p h s t -> p (h s t)"), cvs_out[:].rearrange("p h s t -> p (h s t)"))

    def merge_phase(l, ti, early=None):
        p0, np_, ns, NT, parts = tile_info(ti)
        A = Ar(R3)
        sga = [A.f32(NTMAX) for _ in range(2)]
        sgb = [A.f32(NTMAX) for _ in range(2)]
        t1 = A.f32(NTMAX)
        t2 = A.f32(NTMAX)
        assert A.o <= P1BASE
        for m in range(KT):
            pga = proj(16, lambda k: xn[:, k, :], np_, ns, parts)
            for pi_, (c0, c1) in enumerate(parts):
                act(sga[m % 2][:, c0:c1], pga[pi_], AF.Sigmoid, bias=smv("b_in", 48 + m))
            pgb = proj(16, lambda k: xn[:, k, :], np_, ns, parts)
            for pi_, (c0, c1) in enumerate(parts):
                act(sgb[m % 2][:, c0:c1], pgb[pi_], AF.Sigmoid, bias=smv("b_in", 64 + m))
            if early is not None:
                next(early, None)
            pa = proj(8, lambda k: ya[:, k, :], np_, ns, parts)
            for pi_, (c0, c1) in enumerate(parts):
                tt(t1[:, c0:c1], sga[m % 2][:, c0:c1], pa[pi_], ALU.mult)
            pb = proj(16, lambda k: yb[:, k, :], np_, ns, parts)
            for pi_, (c0, c1) in enumerate(parts):
                tt(t2[:, c0:c1], sgb[m % 2][:, c0:c1], pb[pi_], ALU.mult)
            tt(merged[:, m, 0:NT], t1[:, 0:NT], t2[:, 0:NT], ALU.add)

    def out_phase(l, ti, early=None):
        p0, np_, ns, NT, parts = tile_info(ti)
        srcp, srcs = xsrc(l)
        A = Ar(R3)
        NXR = 4
        xr = [A.f32(NTMAX) for _ in range(NXR)]
        xo = [A.f32(NTMAX) for _ in range(2)]
        tmo = A.f32(NTMAX)
        assert A.o <= P1BASE

        def reload(m):
            dma("sp", xr[m % NXR][:, 0:np_], srcp[:, m, p0:p0 + np_])
            if ns:
                dma("sp", xr[m % NXR][:, np_:NT], srcs[:, m, :])
        for m in range(min(NXR - 1, KT)):
            reload(m)
        for m in range(KT):
            if early is not None:
                next(early, None)
            if m + NXR - 1 < KT:
                reload(m + NXR - 1)
            po = proj(16, lambda k: merged[:, k, :], np_, ns, parts)
            r_, o_ = xr[m % NXR], xo[m % 2]
            stt(o_[:, 0:np_], po[0], ada[:, l, 32 + m, 0:1], r_[:, 0:np_], ALU.mult, ALU.add)
            if ns:
                tt(sview(tmo[:, np_:NT]), sview(po[1]),
                   ada[:, l, 32 + m, 1:1 + ns].unsqueeze(2).to_broadcast([128, ns, 4]), ALU.mult)
                tt(o_[:, np_:NT], tmo[:, np_:NT], r_[:, np_:NT], ALU.add)
            dma("sp", xscr[:, m, p0:p0 + np_], o_[:, 0:np_])
            if ns:
                dma("sp", xscr[:, m, PT:TOK], o_[:, np_:NT])

    def final_norm_stream(ti):
        p0, np_, ns, NT, parts = tile_info(ti)
        assert ns == 0
        A = Ar(R1 - 24 * NTMAX)
        xk = [A.f32(NTMAX) for _ in range(2)]
        sq = [A.bf(NTMAX) for _ in range(2)]
        rstd = A.f32(NTMAX)
        yo = [A.f32(NTMAX) for _ in range(2)]
        assert A.o <= R1
        pt = psb[5][:, 0:np_]

        def load(k):
            dma("sp", xk[k % 2][:, 0:np_], xscr[:, k, p0:p0 + np_])
        load(0)
        for k in range(KT):
            if k + 1 < KT:
                load(k + 1)
            act(sq[k % 2][:, 0:np_], xk[k % 2][:, 0:np_], AF.Square)
            mm(pt, ones_bf[:], sq[k % 2][:, 0:np_], k == 0, k == KT - 1)
            yield
        act(rstd[:, 0:np_], pt, AF.Sqrt, scale=1.0 / D, bias=EPS)
        recip(rstd[:, 0:np_], rstd[:, 0:np_])
        load(0)
        for k in range(KT):
            if k + 1 < KT:
                load(k + 1)
            stt(yo[k % 2][:, 0:np_], xk[k % 2][:, 0:np_], fg_sb[:, k:k + 1], rstd[:, 0:np_], ALU.mult, ALU.mult)
            dma("sp", yp[:, k, p0:p0 + np_], yo[k % 2][:, 0:np_])
            yield

    def final_norm(ti):
        p0, np_, ns = tiles[ti]
        NT = np_ + 4 * ns
        parts = [(0, np_)] + ([(np_, NT)] if ns else [])
        HALF = KT * NTMAX * 4
        tmpsz = 2 * NTMAX * 2 + 3 * NTMAX * 4
        if 2 * HALF + 2 * tmpsz <= ARENA * 4:
            base = (ti % 2) * (HALF + tmpsz)
        else:
            base = 0
        A = Ar(base)
        xbuf = A.f32(KT * NTMAX).rearrange("p (k t) -> p k t", t=NTMAX)
        sq = [A.bf(NTMAX) for _ in range(2)]
        rstd = A.f32(NTMAX)
        yo = [A.f32(NTMAX) for _ in range(2)]
        dma("sp", xbuf[:, :, 0:np_], xscr[:, :, p0:p0 + np_])
        if ns:
            dma("sp", xbuf[:, :, np_:NT], xscr[:, :, PT:TOK])
        pt = [ps_main()[:, 0:np_]] + ([ps_samp()[:, 0:4 * ns]] if ns else [])
        for k in range(KT):
            act(sq[k % 2][:, 0:NT], xbuf[:, k, 0:NT], AF.Square)
            for pi_, (c0, c1) in enumerate(parts):
                mm(pt[pi_], ones_bf[:], sq[k % 2][:, c0:c1], k == 0, k == KT - 1)
        for pi_, (c0, c1) in enumerate(parts):
            act(rstd[:, c0:c1], pt[pi_], AF.Sqrt, scale=1.0 / D, bias=EPS)
        recip(rstd[:, 0:NT], rstd[:, 0:NT])
        for k in range(KT):
            o_ = yo[k % 2]
            stt(o_[:, 0:NT], xbuf[:, k, 0:NT], fg_sb[:, k:k + 1], rstd[:, 0:NT], ALU.mult, ALU.mult)
            dma("sp", yp[:, k, p0:p0 + np_], o_[:, 0:np_])
            if ns:
                dma("sp", ys[:, k, :], o_[:, np_:NT])

    for l in range(depth):
        s5_prep(l, part="a")
        ada_phase(l)
        s5_prep(l, part="b")
    assert len(tiles) >= 2
    seq = [(l, ti) for l in range(depth) for ti in range(len(tiles))]
    layer_setup(0)
    run_all(norm_pass1(0, 0, 0))
    run_all(norm_pass2(0, 0, 0))
    for idx, (l, ti) in enumerate(seq):
        nxt = seq[idx + 1] if idx + 1 < len(seq) else None
        par = (idx + 1) % 2
        fin = None
        if l == depth - 1 and ti >= 1 and tiles[ti - 1][2] == 0:
            fin = final_norm_stream(ti - 1)
        s5_phase(l, ti, extra=fin)
        rg_phase(l, ti)
        if nxt is not None:
            g1_ = norm_pass1(nxt[0], nxt[1], par)
            merge_phase(l, ti, early=g1_)
            run_all(g1_)
            if nxt[0] != l:
                layer_setup(nxt[0])
            g2_ = norm_pass2(nxt[0], nxt[1], par)
            out_phase(l, ti, early=g2_)
            run_all(g2_)
        else:
            merge_phase(l, ti)
            out_phase(l, ti)
    final_norm(len(tiles) - 1)

    P.emit(stack)
    stack.close()
    return nc, P


FULL_CFG = {"depth": DEPTH, "P": 2048, "NS": 16,
            "tiles": [(0, 512, 0), (512, 512, 0), (1024, 512, 0), (1536, 512, 16)]}


def make_in_maps(inp, cfg, prompt_rows, sample_rows):
    depth, PT, NS = cfg["depth"], cfg["P"], cfg["NS"]
    ncmax = max(np_ // LCH for (_, np_, ns) in cfg["tiles"])
    consts = make_consts(ncmax)
    f = np.float32
    shared = {}
    shared["wst"] = np.stack([pack_layer_weights(inp["w_in"][l], inp["s5_w_glu"][l], inp["w_proj_a"][l],
                                                 inp["w_proj_b"][l], inp["w_out"][l]) for l in range(depth)])
    wa = np.asarray(inp["w_ada"][:depth], f).reshape(depth, 16, 128, 48, 128)
    shared["wada"] = np.ascontiguousarray(wa.transpose(0, 2, 3, 1, 4)).reshape(depth, 128, 48 * 2048)
    sml, s5l = zip(*[pack_small(inp, l) for l in range(depth)])
    shared["smd"] = np.stack(sml)
    shared["s5rd"] = np.stack(s5l)
    shared["bada"] = np.ascontiguousarray(
        np.asarray(inp["b_ada"][:depth], f).reshape(depth, 48, 128).transpose(2, 0, 1))
    shared["fgain"] = fm16(np.asarray(inp["final_gain"], f))
    rw = np.stack([np.asarray(inp["rg_w_r"][:depth], f), np.asarray(inp["rg_w_i"][:depth], f)], axis=1)
    shared["rgwd"] = np.ascontiguousarray(rw.transpose(0, 3, 1, 2, 4)).reshape(depth, 128, 2 * NH * 128)
    shared["c_ones"] = consts["ones"]
    shared["c_ident"] = consts["ident"]
    shared["c_cmask"] = consts["cmask"]
    shared["c_selF"] = consts["selF"].reshape(128, 8 * 240)
    shared["c_selB"] = consts["selB"].reshape(128, 8 * 240)
    shared["c_kidx"] = consts["kidx"]
    shared["c_id2"] = consts["id2"]
    maps = []
    for c in range(len(prompt_rows)):
        m = dict(shared)
        b = prompt_rows[c]
        srows = sample_rows[c]
        if b is not None:
            xpr = np.asarray(inp["x_prompt"][b][:PT], f)
            cpr = np.asarray(inp["c_prompt"][b], f)
        else:
            xpr = np.zeros((PT, D), f)
            cpr = np.zeros((D,), f)
        m["xp"] = np.ascontiguousarray(xpr.reshape(PT, KT, 128).transpose(2, 1, 0))
        xsr = np.asarray(inp["x_sample"][srows], f).reshape(NS * 4, KT, 128)
        m["xs"] = np.ascontiguousarray(xsr.transpose(2, 1, 0))
        call = np.concatenate([cpr[None, :], np.asarray(inp["c_sample"][srows], f)], axis=0)
        m["cT"] = np.ascontiguousarray(call.reshape(1 + NS, KT, 128).transpose(2, 1, 0))
        sre = np.asarray(inp["state_s5_re"][:depth][:, srows], f)
        sim = np.asarray(inp["state_s5_im"][:depth][:, srows], f)
        st = np.stack([sre, sim], axis=1)
        st = st.reshape(depth, 2, NS, NQ, 2, NP)
        m["st_s5"] = np.ascontiguousarray(st.transpose(4, 5, 0, 1, 3, 2)).reshape(128, depth, 2, NQ, NS)
        sh = np.asarray(inp["state_rglru_h"][:depth][:, srows], f).reshape(depth, NS, NH, 128)
        m["st_h"] = np.ascontiguousarray(sh.transpose(3, 0, 2, 1))
        sc = np.asarray(inp["state_conv"][:depth][:, srows], f).reshape(depth, NS, 3, NH, 128)
        m["st_cv"] = np.ascontiguousarray(sc.transpose(4, 0, 3, 1, 2))
        maps.append(m)
    return maps


def unpack_core(r, cfg):
    depth, PT, NS = cfg["depth"], cfg["P"], cfg["NS"]
    o = {}
    o["yp"] = r["yp"].transpose(2, 1, 0).reshape(PT, D)
    o["ys"] = r["ys"].transpose(2, 1, 0).reshape(NS, 4, D)
    s5p = r["o_s5p"].reshape(2, NP, depth, 2, NQ)
    s5p = s5p.transpose(2, 3, 4, 0, 1).reshape(depth, 2, NG, NP)
    o["s5p_re"], o["s5p_im"] = s5p[:, 0], s5p[:, 1]
    o["hp"] = r["o_hp"].transpose(1, 2, 0).reshape(depth, RGW)
    o["cvp"] = r["o_cvp"].transpose(1, 3, 2, 0).reshape(depth, 3, RGW)
    s5s = r["o_s5s"].reshape(2, NP, depth, 2, NQ, NS)
    s5s = s5s.transpose(2, 3, 5, 4, 0, 1).reshape(depth, 2, NS, NG, NP)
    o["s5s_re"], o["s5s_im"] = s5s[:, 0], s5s[:, 1]
    o["hs"] = r["o_hs"].transpose(1, 3, 2, 0).reshape(depth, NS, RGW)
    o["cvs"] = r["o_cvs"].transpose(1, 3, 4, 2, 0).reshape(depth, NS, 3, RGW)
    return o


_CACHE = {}


def kernel(**inputs):
    cfg = FULL_CFG
    inp = {k: np.asarray(v) for k, v in inputs.items()}
    B = inp["x_prompt"].shape[0]
    NS = cfg["NS"]
    prompt_rows = [c if c < B else None for c in range(N_CORES)]
    sample_rows = [list(range(c * NS, (c + 1) * NS)) for c in range(N_CORES)]
    maps = make_in_maps(inp, cfg, prompt_rows, sample_rows)
    if "nc" not in _CACHE:
        _CACHE["nc"] = build(cfg)[0]
    nc = _CACHE["nc"]
    res = run_bass_kernel_spmd(nc, maps, core_ids=list(range(N_CORES)))
    outs = [unpack_core(r, cfg) for r in res.results]
    f = np.float32
    y_prompt = np.stack([outs[b]["yp"] for b in range(B)]).astype(f)
    y_sample = np.concatenate([o["ys"] for o in outs], axis=0).astype(f)
    s5_re_p = np.stack([outs[b]["s5p_re"] for b in range(B)], axis=1).astype(f)
    s5_im_p = np.stack([outs[b]["s5p_im"] for b in range(B)], axis=1).astype(f)
    rg_h_p = np.stack([outs[b]["hp"] for b in range(B)], axis=1).astype(f)
    conv_p = np.stack([outs[b]["cvp"] for b in range(B)], axis=1).astype(f)
    s5_re_s = np.concatenate([o["s5s_re"] for o in outs], axis=1).astype(f)
    s5_im_s = np.concatenate([o["s5s_im"] for o in outs], axis=1).astype(f)
    rg_h_s = np.concatenate([o["hs"] for o in outs], axis=1).astype(f)
    conv_s = np.concatenate([o["cvs"] for o in outs], axis=1).astype(f)
    return (y_prompt, y_sample, s5_re_p, s5_im_p, rg_h_p, conv_p, s5_re_s, s5_im_s, rg_h_s, conv_s)
```
